# Optimizing a Trainium2 kernel written in Bass

```python
import math
import jax, jax.numpy as jnp
from jax import lax
import numpy as np

D_MODEL = 1024
BATCH = 8
SEQ = 4096
DEPTH = 1

ATTN_HEAD_DIM = 64
ATTN_PATTERNS = ((128, 1), (512, 4), (2048, 16))
HEADS_PER_PATTERN = 8
N_ATTN_HEADS = HEADS_PER_PATTERN * len(ATTN_PATTERNS)
ATTN_QKV = N_ATTN_HEADS * ATTN_HEAD_DIM
ATTN_OUT = HEADS_PER_PATTERN * ATTN_HEAD_DIM
ATTN_BLOCK = 128
ALIBI_MAX_EXP = 8.0
SSD_EXPAND = 2
SSD_INNER = SSD_EXPAND * D_MODEL
SSD_HEAD_DIM = 64
SSD_HEADS = SSD_INNER // SSD_HEAD_DIM
SSD_STATE = 128
SSD_GROUPS = 4
SSD_CONV = 4
SSD_CHUNK = 128
SSD_CONV_DIM = SSD_INNER + 2 * SSD_GROUPS * SSD_STATE
D_FF = 2816
EPS = 1e-6
IN_COLS = 3 * ATTN_QKV + SSD_INNER + SSD_CONV_DIM + SSD_HEADS + 2 * D_MODEL

kernel_name = "hybrid_dilated_attn_ssd_macaron"


def rmsnorm(x, g):
    x32 = x.astype(jnp.float32)
    y = x32 * lax.rsqrt(jnp.mean(x32 * x32, axis=-1, keepdims=True) + EPS)
    return (y * g.astype(jnp.float32)).astype(x.dtype)


def swiglu(h, w_gate, w_up, w_down):
    return (jax.nn.silu(h @ w_gate) * (h @ w_up)) @ w_down


def alibi_slopes(n):
    return jnp.exp2(-ALIBI_MAX_EXP * jnp.arange(1, n + 1, dtype=jnp.float32) / n)


def dilated_window_attention(q, k, v, slopes, window, dilation):
    b, S, H, hd = q.shape
    L = S // dilation
    n_back = window // dilation
    nb = -(-L // ATTN_BLOCK)
    Lp = nb * ATTN_BLOCK

    def to_blocks(t):
        t = t.reshape(b, L, dilation, H, hd).transpose(0, 2, 1, 3, 4)
        t = jnp.pad(t, ((0, 0), (0, 0), (0, Lp - L), (0, 0), (0, 0)))
        return t.reshape(b, dilation, nb, ATTN_BLOCK, H, hd)

    def with_prev(t):
        prev = jnp.pad(t, ((0, 0), (0, 0), (1, 0), (0, 0), (0, 0), (0, 0)))[:, :, :-1]
        return jnp.concatenate([prev, t], axis=3)

    qb = to_blocks(q)
    kk = with_prev(to_blocks(k))
    vv = with_prev(to_blocks(v))
    scale = 1.0 / math.sqrt(hd)
    logits = jnp.einsum('bdnqhe,bdnkhe->bdnhqk', qb, kk).astype(jnp.float32) * scale

    a_idx = jnp.arange(ATTN_BLOCK)[:, None]
    c_idx = jnp.arange(2 * ATTN_BLOCK)[None, :]
    rel = ATTN_BLOCK + a_idx - c_idx
    band = (rel >= 0) & (rel <= n_back)
    key_pos = jnp.arange(nb)[:, None] * ATTN_BLOCK + jnp.arange(2 * ATTN_BLOCK)[None, :] - ATTN_BLOCK
    mask = band[None] & (key_pos >= 0)[:, None, :]
    bias = -slopes[:, None, None] * (rel * dilation).astype(jnp.float32)[None]
    logits = jnp.where(mask[None, None, :, None], logits + bias[None, None, None], -jnp.inf)

    m = jnp.max(logits, axis=-1, keepdims=True)
    p = jnp.exp(logits - m)
    l = jnp.sum(p, axis=-1, keepdims=True)
    o = jnp.einsum('bdnhqk,bdnkhe->bdnqhe', p / l, vv.astype(jnp.float32))
    lse = (m + jnp.log(l))[..., 0]

    o = o.reshape(b, dilation, Lp, H, hd)[:, :, :L].transpose(0, 2, 1, 3, 4).reshape(b, S, H, hd)
    lse = lse.transpose(0, 1, 2, 4, 3).reshape(b, dilation, Lp, H)[:, :, :L]
    lse = lse.transpose(0, 2, 1, 3).reshape(b, S, H)
    return o, lse


def attention_branch(q, k, v, q_gain, k_gain):
    b, S, _ = q.shape
    q = rmsnorm(q.reshape(b, S, N_ATTN_HEADS, ATTN_HEAD_DIM), q_gain)
    k = rmsnorm(k.reshape(b, S, N_ATTN_HEADS, ATTN_HEAD_DIM), k_gain)
    v = v.reshape(b, S, N_ATTN_HEADS, ATTN_HEAD_DIM)
    slopes = alibi_slopes(N_ATTN_HEADS)
    outs, lses = [], []
    for g, (window, dilation) in enumerate(ATTN_PATTERNS):
        hs = slice(g * HEADS_PER_PATTERN, (g + 1) * HEADS_PER_PATTERN)
        o, lse = dilated_window_attention(q[:, :, hs], k[:, :, hs], v[:, :, hs],
                                          slopes[hs], window, dilation)
        outs.append(o)
        lses.append(lse)
    w = jax.nn.softmax(jnp.stack(lses, axis=0), axis=0)
    o = jnp.sum(w[..., None] * jnp.stack(outs, axis=0), axis=0)
    return o.reshape(b, S, ATTN_OUT).astype(q.dtype)


def causal_depthwise_conv(u, w, bias):
    C = u.shape[-1]
    y = lax.conv_general_dilated(u, w[:, None, :], window_strides=(1,),
                                 padding=[(SSD_CONV - 1, 0)],
                                 dimension_numbers=('NWC', 'WIO', 'NWC'),
                                 feature_group_count=C)
    return y + bias


def ssd_chunked(x, a, Bm, Cm):
    b, S, H, P = x.shape
    G, N = Bm.shape[2], Bm.shape[3]
    J = H // G
    nc = S // SSD_CHUNK
    Q = SSD_CHUNK
    x = x.reshape(b, nc, Q, G, J, P)
    a = a.reshape(b, nc, Q, G, J).transpose(0, 1, 3, 4, 2)
    Bm = Bm.reshape(b, nc, Q, G, N)
    Cm = Cm.reshape(b, nc, Q, G, N)
    a_cs = jnp.cumsum(a, axis=-1)
    diff = a_cs[..., :, None] - a_cs[..., None, :]
    causal = jnp.tril(jnp.ones((Q, Q), dtype=bool))
    decay = jnp.exp(jnp.where(causal, diff, -jnp.inf))
    CB = jnp.einsum('bclgn,bcsgn->bcgls', Cm, Bm)
    y_diag = jnp.einsum('bcgjls,bcsgjp->bclgjp', CB[:, :, :, None] * decay, x)
    decay_states = jnp.exp(a_cs[..., -1:] - a_cs)
    states = jnp.einsum('bclgn,bcgjl,bclgjp->bcgjpn', Bm, decay_states, x)
    chunk_decay = jnp.exp(a_cs[..., -1])

    def step(carry, inp):
        st, dec = inp
        return carry * dec[..., None, None] + st, carry

    init = jnp.zeros((b, G, J, P, N), dtype=x.dtype)
    _, prev_states = lax.scan(step, init, (states.transpose(1, 0, 2, 3, 4, 5),
                                           chunk_decay.transpose(1, 0, 2, 3)))
    prev_states = prev_states.transpose(1, 0, 2, 3, 4, 5)
    y_off = jnp.einsum('bclgn,bcgjpn,bcgjl->bclgjp', Cm, prev_states, jnp.exp(a_cs))
    return (y_diag + y_off).reshape(b, S, H, P)


def ssd_branch(z, xBC, dt_raw, conv_w, conv_b, dt_bias, a_log, d_skip, ssd_norm):
    b, S, _ = z.shape
    xBC = jax.nn.silu(causal_depthwise_conv(xBC, conv_w, conv_b))
    xs, Bm, Cm = jnp.split(xBC, [SSD_INNER, SSD_INNER + SSD_GROUPS * SSD_STATE], axis=-1)
    xs = xs.reshape(b, S, SSD_HEADS, SSD_HEAD_DIM).astype(jnp.float32)
    Bm = Bm.reshape(b, S, SSD_GROUPS, SSD_STATE).astype(jnp.float32)
    Cm = Cm.reshape(b, S, SSD_GROUPS, SSD_STATE).astype(jnp.float32)
    dt = jax.nn.softplus(dt_raw.astype(jnp.float32) + dt_bias.astype(jnp.float32))
    A = -jnp.exp(a_log.astype(jnp.float32))
    y = ssd_chunked(xs * dt[..., None], dt * A, Bm, Cm)
    y = y + xs * d_skip.astype(jnp.float32)[:, None]
    y = y.reshape(b, S, SSD_INNER).astype(z.dtype) * jax.nn.silu(z)
    y = rmsnorm(y.reshape(b, S, SSD_GROUPS, SSD_INNER // SSD_GROUPS),
                ssd_norm.reshape(SSD_GROUPS, SSD_INNER // SSD_GROUPS))
    return y.reshape(b, S, SSD_INNER)


def setup_inputs(seed: int = 0) -> dict:
    key = jax.random.key(seed)
    ks = jax.random.split(key, 24)
    f32 = jnp.float32

    def normal(k, shape, scale):
        return jax.random.normal(k, shape, f32) * scale

    def gain(k, n):
        return 1.0 + 0.02 * jax.random.normal(k, (DEPTH, n), f32)

    dt = jnp.exp(jax.random.uniform(ks[12], (DEPTH, SSD_HEADS), f32,
                                    minval=math.log(1e-3), maxval=math.log(1e-1)))
    return {
        "x": normal(ks[0], (BATCH, SEQ, D_MODEL), 1.0),
        "ffn1_norm": gain(ks[1], D_MODEL),
        "ffn1_w_gate": normal(ks[2], (DEPTH, D_MODEL, D_FF), D_MODEL ** -0.5),
        "ffn1_w_up": normal(ks[3], (DEPTH, D_MODEL, D_FF), D_MODEL ** -0.5),
        "ffn1_w_down": normal(ks[4], (DEPTH, D_FF, D_MODEL), D_FF ** -0.5),
        "mix_norm": gain(ks[5], D_MODEL),
        "w_in": normal(ks[6], (DEPTH, D_MODEL, IN_COLS), D_MODEL ** -0.5),
        "q_norm": gain(ks[7], ATTN_HEAD_DIM),
        "k_norm": gain(ks[8], ATTN_HEAD_DIM),
        "conv_w": normal(ks[9], (DEPTH, SSD_CONV, SSD_CONV_DIM), SSD_CONV ** -0.5),
        "conv_b": normal(ks[10], (DEPTH, SSD_CONV_DIM), 0.02),
        "dt_bias": dt + jnp.log(-jnp.expm1(-dt)),
        "a_log": jnp.log(jax.random.uniform(ks[13], (DEPTH, SSD_HEADS), f32, minval=1.0, maxval=16.0)),
        "d_skip": 1.0 + 0.1 * jax.random.normal(ks[14], (DEPTH, SSD_HEADS), f32),
        "ssd_norm": gain(ks[15], SSD_INNER),
        "w_attn_branch": normal(ks[16], (DEPTH, ATTN_OUT, D_MODEL), ATTN_OUT ** -0.5),
        "w_ssd_branch": normal(ks[17], (DEPTH, SSD_INNER, D_MODEL), SSD_INNER ** -0.5),
        "w_out": normal(ks[18], (DEPTH, D_MODEL, D_MODEL), D_MODEL ** -0.5),
        "ffn2_norm": gain(ks[19], D_MODEL),
        "ffn2_w_gate": normal(ks[20], (DEPTH, D_MODEL, D_FF), D_MODEL ** -0.5),
        "ffn2_w_up": normal(ks[21], (DEPTH, D_MODEL, D_FF), D_MODEL ** -0.5),
        "ffn2_w_down": normal(ks[22], (DEPTH, D_FF, D_MODEL), D_FF ** -0.5),
    }


def reference(x, ffn1_norm, ffn1_w_gate, ffn1_w_up, ffn1_w_down, mix_norm, w_in,
              q_norm, k_norm, conv_w, conv_b, dt_bias, a_log, d_skip, ssd_norm,
              w_attn_branch, w_ssd_branch, w_out, ffn2_norm, ffn2_w_gate, ffn2_w_up,
              ffn2_w_down):
    split_at = np.cumsum([ATTN_QKV, ATTN_QKV, ATTN_QKV, SSD_INNER, SSD_CONV_DIM,
                          SSD_HEADS, D_MODEL]).tolist()
    for l in range(DEPTH):
        x = x + 0.5 * swiglu(rmsnorm(x, ffn1_norm[l]), ffn1_w_gate[l], ffn1_w_up[l], ffn1_w_down[l])
        h = rmsnorm(x, mix_norm[l])
        proj = h @ w_in[l]
        q, k, v, z, xBC, dt_raw, g_attn, g_ssd = jnp.split(proj, split_at, axis=-1)
        a = attention_branch(q, k, v, q_norm[l], k_norm[l]) @ w_attn_branch[l]
        s = ssd_branch(z, xBC, dt_raw, conv_w[l], conv_b[l], dt_bias[l], a_log[l],
                       d_skip[l], ssd_norm[l]) @ w_ssd_branch[l]
        merged = jax.nn.sigmoid(g_attn) * a + jax.nn.sigmoid(g_ssd) * s
        x = x + merged @ w_out[l]
        x = x + 0.5 * swiglu(rmsnorm(x, ffn2_norm[l]), ffn2_w_gate[l], ffn2_w_up[l], ffn2_w_down[l])
    return x
```

```python
import math
from contextlib import ExitStack
import numpy as np
import concourse.bass as bass
import concourse.mybir as mybir
from concourse.bass_utils import run_bass_kernel_spmd

F32 = mybir.dt.float32
BF16 = mybir.dt.bfloat16
I32 = mybir.dt.int32
AF = mybir.ActivationFunctionType
ALU = mybir.AluOpType

S = 4096
D = 1024
DFF = 2816
NCOLS = 11808
QOFF, KOFF, VOFF, ZOFF, XOFF, BOFF, COFF, DTOFF, GAOFF, GSOFF = 0, 1536, 3072, 4608, 6656, 8704, 9216, 9728, 9760, 10784
EPS = 1e-6
V_N1, V_NM, V_N2, V_GQ, V_GK, V_CW, V_CB, V_DSK, V_SSN, NV = 0, 8, 16, 24, 25, 26, 122, 146, 162, 178
BIG = 1.0e6
ATT = [4, 3, 9]


class Buf:
    __slots__ = ("name", "t", "last_w", "reads", "dsem", "dcnt", "is_psum")

    def __init__(self, name, t):
        self.is_psum = False
        self.name = name
        self.t = t
        self.last_w = None
        self.reads = {}
        self.dsem = None
        self.dcnt = 0

    def __getitem__(self, idx):
        return self.t[idx]


class KB:
    def __init__(self, nc, stack):
        self.nc = nc
        self.stack = stack
        self.engs = {"pe": nc.tensor, "act": nc.scalar, "dve": nc.vector, "pool": nc.gpsimd, "sp": nc.sync}
        self.sems = {}
        self.ecnt = {}
        for e in ("pe", "act", "dve", "pool"):
            self.sems["E" + e] = stack.enter_context(nc.semaphore("sem_" + e))
            self.ecnt[e] = 0
        self.seen = {e: {} for e in self.engs}
        self.n_inst = 0
        self.n_wait = 0
        self.dcur = {}

    def sbuf(self, name, shape, dtype, st=None):
        t = (st or self.stack).enter_context(self.nc.sbuf_tensor(name, list(shape), dtype))
        return Buf(name, t)

    def psum(self, name, shape, dtype):
        t = self.stack.enter_context(self.nc.psum_tensor(name, list(shape), dtype))
        b = Buf(name, t)
        b.is_psum = True
        return b

    def dram(self, name, shape, dtype, kind="Internal"):
        return Buf(name, self.nc.dram_tensor(name, list(shape), dtype, kind=kind))

    def _wait(self, eng, ev):
        if ev is None:
            return
        key, val = ev
        if self.seen[eng].get(key, 0) >= val:
            return
        self.engs[eng].wait_ge(self.sems[key], val)
        self.seen[eng][key] = val
        self.n_wait += 1

    def _deps(self, eng, reads, writes):
        own = "E" + eng
        for b in reads:
            ev = b.last_w
            if ev is not None and not (ev[0] == own and eng == "pe"):
                self._wait(eng, ev)
        for b in writes:
            ev = b.last_w
            if ev is not None and not (ev[0] == own and eng == "pe"):
                self._wait(eng, ev)
            for key, val in b.reads.items():
                if key != own or eng != "pe":
                    self._wait(eng, (key, val))

    def _commit(self, ev, reads, writes):
        key, val = ev
        for b in reads:
            if b.reads.get(key, 0) < val:
                b.reads[key] = val
        for b in writes:
            b.last_w = ev
            b.reads = {}

    def barrier(self):
        for e in self.engs:
            for key in self.sems:
                if key.startswith("E"):
                    val = self.ecnt[key[1:]]
                    if key == "E" + e:
                        continue
                else:
                    val = self.dcur.get(key, 0)
                if val > 0:
                    self._wait(e, (key, val))

    def op(self, eng, fn, reads=(), writes=(), sig=True):
        if eng != "pe":
            pr = [b for b in reads if b.is_psum]
            if pr:
                writes = list(writes) + [b for b in pr if b not in writes]
        self._deps(eng, reads, writes)
        inst = fn(self.engs[eng])
        self.n_inst += 1
        key = "E" + eng
        if sig:
            self.ecnt[eng] += 1
            inst.then_inc(self.sems[key], 1)
            ev = (key, self.ecnt[eng])
        else:
            ev = (key, self.ecnt[eng] + 1)
        self._commit(ev, reads, writes)

    def dma(self, q, out_buf, out_ap, in_buf, in_ap):
        self._deps(q, [in_buf], [out_buf])
        if out_buf.dsem is None:
            key = "D" + out_buf.name
            self.sems[key] = self.stack.enter_context(self.nc.semaphore("dsem_" + out_buf.name))
            out_buf.dsem = key
        out_buf.dcnt += 16
        self.dcur[out_buf.dsem] = out_buf.dcnt
        inst = self.engs[q].dma_start(out=out_ap, in_=in_ap)
        inst.then_inc(self.sems[out_buf.dsem], 16)
        self.n_inst += 1
        self._commit((out_buf.dsem, out_buf.dcnt), [in_buf], [out_buf])

    def finish(self, eng, bufs):
        for b in bufs:
            self._wait(eng, b.last_w)


def mm(k, out_buf, out_ap, l_buf, l_ap, r_buf, r_ap, start, stop):
    k.op("pe", lambda e: e.matmul(out_ap, l_ap, r_ap, start=start, stop=stop),
         reads=[l_buf, r_buf], writes=[out_buf], sig=stop)


def b3(ap, shape, axis):
    return ap.unsqueeze(axis).broadcast_to(shape)


def rmsnorm_fm(k, C, X, nchunk, gcol, out_buf, out_fn, sq, lnv, rstd, pss, dnorm):
    k.op("act", lambda e: e.activation(out=sq[:, 0:nchunk, :], in_=X[:, 0:nchunk, :], func=AF.Square),
         reads=[X], writes=[sq])
    for c in range(nchunk):
        mm(k, pss, pss[:, :], C["ones"], C["ones"][:, :], sq, sq[:, c, :], c == 0, c == nchunk - 1)
    k.op("act", lambda e: e.activation(out=lnv[:, :], in_=pss[:, :], func=AF.Ln, scale=1.0 / dnorm, bias=C["eps"][:, 0:1]),
         reads=[pss, C["eps"]], writes=[lnv])
    k.op("act", lambda e: e.activation(out=rstd[:, :], in_=lnv[:, :], func=AF.Exp, scale=-0.5),
         reads=[lnv], writes=[rstd])
    vec = C["vec"]
    for c in range(nchunk):
        k.op("dve", lambda e: e.scalar_tensor_tensor(out=out_fn(c), in0=X[:, c, :], scalar=vec[:, gcol + c:gcol + c + 1],
                                                     in1=rstd[:, :], op0=ALU.mult, op1=ALU.mult),
             reads=[X, vec, rstd], writes=[out_buf])


def norm_a(k, X, out_buf, out_fn):
    for c in range(8):
        k.op("act", lambda e: e.activation(out=out_fn(c), in_=X[:, c, :], func=AF.Square), reads=[X], writes=[out_buf])


def norm_b(k, C, X, gcol, out_buf, out_fn, lnv, rstd, pss):
    for c in range(8):
        mm(k, pss, pss[:, :], C["ones"], C["ones"][:, :], out_buf, out_fn(c), c == 0, c == 7)
    k.op("act", lambda e: e.activation(out=lnv[:, :], in_=pss[:, :], func=AF.Ln, scale=1.0 / 1024, bias=C["eps"][:, 0:1]),
         reads=[pss, C["eps"]], writes=[lnv])
    k.op("act", lambda e: e.activation(out=rstd[:, :], in_=lnv[:, :], func=AF.Exp, scale=-0.5),
         reads=[lnv], writes=[rstd])
    vec = C["vec"]
    for c in range(8):
        k.op("dve", lambda e: e.scalar_tensor_tensor(out=out_fn(c), in0=X[:, c, :], scalar=vec[:, gcol + c:gcol + c + 1],
                                                     in1=rstd[:, :], op0=ALU.mult, op1=ALU.mult),
             reads=[X, vec, rstd], writes=[out_buf])


def ffn_phase(k, C, P, name, src, dst, wg, wu, wd, ncol, mix=None):
    k.barrier()
    with ExitStack() as st:
        xg = [k.sbuf(f"{name}_xg{i}", [128, 8, 512], F32, st) for i in range(2)]
        h1 = [k.sbuf(f"{name}_h1{i}", [128, 8, 512], BF16, st) for i in range(2)]
        act = k.sbuf(name + "_act", [128, 22, 512], BF16, st)
        gw = [k.sbuf(f"{name}_gw{i}", [128, 8, 512], BF16, st) for i in range(2)]
        uw = [k.sbuf(f"{name}_uw{i}", [128, 8, 512], BF16, st) for i in range(2)]
        dw = [k.sbuf(f"{name}_dw{i}", [128, 22, 256], BF16, st) for i in range(2)]
        sg = [k.sbuf(f"{name}_sg{i}", [128, 512], BF16, st) for i in range(2)]
        lnv = [k.sbuf(f"{name}_lnv{i}", [128, 512], F32, st) for i in range(2)]
        rstd = [k.sbuf(f"{name}_rstd{i}", [128, 512], F32, st) for i in range(2)]
        srcv = src.t.ap().rearrange("(c p) t -> p c t", p=128)
        dstv = dst.t.ap().rearrange("(c p) t -> p c t", p=128)
        wgv = wg.t.ap().rearrange("(c p) m -> p c m", p=128)
        wuv = wu.t.ap().rearrange("(c p) m -> p c m", p=128)
        wdv = wd.t.ap().rearrange("(c p) m -> p c m", p=128)
        k.dma("sp", xg[0], xg[0][:, :, :], src, srcv[:, :, 0:512])
        H0 = h1[0]
        norm_a(k, xg[0], H0, lambda c: H0[:, c, :])
        norm_b(k, C, xg[0], ncol, H0, lambda c: H0[:, c, :], lnv[0], rstd[0], P[6])
        wcnt = 0
        dcnt = 0
        pend_mix = None
        for G in range(8):
            X = xg[G % 2]
            H = h1[G % 2]
            tc = slice(G * 512, (G + 1) * 512)
            if G + 1 < 8:
                Xn = xg[(G + 1) % 2]
                Hn = h1[(G + 1) % 2]
            for jb in range(6):
                cols = 512 if jb < 5 else 256
                ws = wcnt % 2
                wcnt += 1
                k.dma("pool", gw[ws], gw[ws][:, :, 0:cols], wg, wgv[:, :, jb * 512:jb * 512 + cols])
                k.dma("pool", uw[ws], uw[ws][:, :, 0:cols], wu, wuv[:, :, jb * 512:jb * 512 + cols])
                if jb == 4 and G + 1 < 8:
                    norm_a(k, Xn, Hn, lambda c: Hn[:, c, :])
                for jj in range(cols // 128):
                    j = jb * 4 + jj
                    pg, pu = P[(j % 2) * 2], P[(j % 2) * 2 + 1]
                    ms = slice(jj * 128, (jj + 1) * 128)
                    for c in range(8):
                        mm(k, pg, pg[:, :], gw[ws], gw[ws][:, c, ms], H, H[:, c, :], c == 0, c == 7)
                    for c in range(8):
                        mm(k, pu, pu[:, :], uw[ws], uw[ws][:, c, ms], H, H[:, c, :], c == 0, c == 7)
                    sgb = sg[j % 2]
                    k.op("act", lambda e: e.activation(out=sgb[:, :], in_=pg[:, :], func=AF.Silu), reads=[pg], writes=[sgb])
                    k.op("dve", lambda e: e.tensor_tensor(out=act[:, j, :], in0=sgb[:, :], in1=pu[:, :], op=ALU.mult),
                         reads=[sgb, pu], writes=[act])
                if jb == 0:
                    if pend_mix is not None:
                        pend_mix()
                        pend_mix = None
                    if G + 1 < 8:
                        k.dma("sp", Xn, Xn[:, :, :], src, srcv[:, :, (G + 1) * 512:(G + 2) * 512])
            if G + 1 < 8:
                norm_b(k, C, Xn, ncol, Hn, lambda c: Hn[:, c, :], lnv[0], rstd[0], P[6])
            for mp in range(4):
                ds_ = dcnt % 2
                dcnt += 1
                k.dma("pool", dw[ds_], dw[ds_][:, :, :], wd, wdv[:, :, mp * 256:(mp + 1) * 256])
                for mm_ in range(2):
                    m = mp * 2 + mm_
                    pb = P[4 + m % 2]
                    for kc in range(22):
                        mm(k, pb, pb[:, :], dw[ds_], dw[ds_][:, kc, mm_ * 128:(mm_ + 1) * 128], act, act[:, kc, :], kc == 0, kc == 21)
                    k.op("dve", lambda e: e.scalar_tensor_tensor(out=X[:, m, :], in0=pb[:, :], scalar=0.5, in1=X[:, m, :],
                                                                 op0=ALU.mult, op1=ALU.add), reads=[pb, X], writes=[X])
            k.dma("sp", dst, dstv[:, :, tc], X, X[:, :, :])
            if mix is not None:
                mcol, hT = mix
                norm_a(k, X, hT, lambda c, tc=tc: hT[:, c, tc])
                pend_mix = (lambda X=X, tc=tc: norm_b(k, C, X, mcol, hT, lambda c: hT[:, c, tc], lnv[1], rstd[1], P[6]))
        if pend_mix is not None:
            pend_mix()


def attn_phase(k, C, P, hT, w_in, oT_d):
    vec = C["vec"]
    winv = w_in.t.ap().rearrange("(c p) m -> p c m", p=128)
    oTv = oT_d.t.ap().rearrange("(j p) t -> p j t", p=128)
    k.barrier()
    with ExitStack() as st:
        acc = k.sbuf("acc", [128, 2, S], F32, st)
        qs = k.sbuf("qs", [128, S], BF16, st)
        ks = k.sbuf("ks", [128, S], BF16, st)
        vs = k.sbuf("vs", [128, 32, 128], BF16, st)
        wqs = [k.sbuf(f"wq{i}", [128, 8, 128], BF16, st) for i in range(2)]
        wks = [k.sbuf(f"wk{i}", [128, 8, 128], BF16, st) for i in range(2)]
        wvs = [k.sbuf(f"wv{i}", [128, 8, 128], BF16, st) for i in range(2)]
        tmp = [k.sbuf(f"a_tmp{i}", [128, 512], F32, st) for i in range(2)]
        pT = [k.sbuf(f"a_pT{i}", [128, 512], BF16, st) for i in range(2)]
        sqa = [k.sbuf(f"a_sq{i}", [128, 512], BF16, st) for i in range(2)]
        lnq = k.sbuf("a_ln", [128, 512], F32, st)
        rsq = [k.sbuf(f"a_rs{i}", [128, 512], F32, st) for i in range(2)]
        rec = k.sbuf("a_rec", [128, 512], F32, st)
        ob = k.sbuf("a_ob", [128, S], BF16, st)
        hgs = [k.sbuf(f"a_hg{i}", [128, 8, 512], BF16, st) for i in range(2)]
        r32 = [k.sbuf(f"a_r32{i}", [128, 256], F32, st) for i in range(2)]
        rhis = [[k.sbuf(f"a_rhi{j}{i}", [128, 256], BF16, st) for i in range(2)] for j in range(2)]
        rlos = [[k.sbuf(f"a_rlo{j}{i}", [128, 256], BF16, st) for i in range(2)] for j in range(2)]
        blkc = [0]
        it = 0
        for p in range(ATT[0]):
            for g in range(ATT[1]):
                d = (1, 4, 16)[g]
                L = S // d
                hA = 8 * g + 2 * p
                wq, wk, wv = wqs[it % 2], wks[it % 2], wvs[it % 2]
                it += 1
                k.dma("pool", wq, wq[:, :, :], w_in, winv[:, :, QOFF + hA * 64:QOFF + hA * 64 + 128])
                k.dma("pool", wk, wk[:, :, :], w_in, winv[:, :, KOFF + hA * 64:KOFF + hA * 64 + 128])
                k.dma("pool", wv, wv[:, :, :], w_in, winv[:, :, VOFF + hA * 64:VOFF + hA * 64 + 128])
                NB = min(512, L)
                items = []
                gsrc = {}
                gcnt = 0
                for r in range(d):
                    for i0 in range(0, L, NB):
                        items.append((wq, qs, V_GQ, r, i0))
                        items.append((wk, ks, V_GK, r, i0))
                        if d > 1:
                            gsrc[(r, i0)] = hgs[gcnt % 2]
                            gcnt += 1

                def gather(r, i0):
                    hg = gsrc[(r, i0)]
                    t0 = r + d * i0
                    tsl = slice(t0, t0 + d * (NB - 1) + 1, d)
                    k.op("dve", lambda e: e.tensor_copy(hg[:, :, 0:NB], hT[:, :, tsl]), reads=[hT], writes=[hg])

                def proj_mm(ii):
                    w, dstb, gcol, r, i0 = items[ii]
                    t0 = r + d * i0
                    tsl = slice(t0, t0 + d * (NB - 1) + 1, d)
                    pq = P[(0, 1, 3, 4)[ii % 4]]
                    if d > 1:
                        if ii % 2 == 0:
                            gather(r, i0)
                        hg = gsrc[(r, i0)]
                        for c in range(8):
                            mm(k, pq, pq[:, 0:NB], w, w[:, c, :], hg, hg[:, c, 0:NB], c == 0, c == 7)
                    else:
                        for c in range(8):
                            mm(k, pq, pq[:, 0:NB], w, w[:, c, :], hT, hT[:, c, tsl], c == 0, c == 7)
                    sq_ = sqa[ii % 2]
                    k.op("act", lambda e: e.activation(out=sq_[:, 0:NB], in_=pq[:, 0:NB], func=AF.Square),
                         reads=[pq], writes=[sq_])

                def proj_norm(ii):
                    w, dstb, gcol, r, i0 = items[ii]
                    pos0 = r * L + i0
                    pq = P[(0, 1, 3, 4)[ii % 4]]
                    sq_ = sqa[ii % 2]
                    rs_ = rsq[ii % 2]
                    mm(k, P[2], P[2][:, 0:NB], C["bd"], C["bd"][:, :], sq_, sq_[:, 0:NB], True, True)
                    k.op("act", lambda e: e.activation(out=lnq[:, 0:NB], in_=P[2][:, 0:NB], func=AF.Ln, scale=1.0 / 64,
                                                       bias=C["eps"][:, 0:1]), reads=[P[2], C["eps"]], writes=[lnq])
                    k.op("act", lambda e: e.activation(out=rs_[:, 0:NB], in_=lnq[:, 0:NB], func=AF.Exp, scale=-0.5),
                         reads=[lnq], writes=[rs_])
                    k.op("dve", lambda e: e.scalar_tensor_tensor(out=dstb[:, pos0:pos0 + NB], in0=pq[:, 0:NB],
                                                                 scalar=vec[:, gcol:gcol + 1], in1=rs_[:, 0:NB],
                                                                 op0=ALU.mult, op1=ALU.mult),
                         reads=[pq, vec, rs_], writes=[dstb])

                proj_mm(0)
                for ii in range(len(items)):
                    if ii + 1 < len(items):
                        proj_mm(ii + 1)
                    proj_norm(ii)
                vcnt = 0
                for r in range(d):
                    for b in range(L // 128):
                        pv = P[(3, 4, 0, 1)[vcnt % 4]]
                        vcnt += 1
                        tb0 = r + d * b * 128
                        tsb = slice(tb0, tb0 + d * 127 + 1, d)
                        for c in range(8):
                            mm(k, pv, pv[:, 0:128], hT, hT[:, c, tsb], wv, wv[:, c, :], c == 0, c == 7)
                        bi = (r * L) // 128 + b
                        k.op("act", lambda e: e.activation(out=vs[:, bi, :], in_=pv[:, 0:128], func=AF.Copy),
                             reads=[pv], writes=[vs])
                slopes = [2.0 ** (-8.0 * (hA + hh + 1) / 24.0) for hh in range(2)]
                blocks = [(r, n) for r in range(d) for n in range(L // 128)]
                rhi, rlo = rhis[it % 2], rlos[it % 2]
                for hh in range(2):
                    sc = -slopes[hh] * d
                    k.op("dve", lambda e: e.tensor_scalar(out=r32[hh][:, :], in0=C["R2"][:, :], scalar1=sc, scalar2=None,
                                                          op0=ALU.mult), reads=[C["R2"]], writes=[r32[hh]])
                    k.op("dve", lambda e: e.tensor_copy(rhi[hh][:, :], r32[hh][:, :]), reads=[r32[hh]], writes=[rhi[hh]])
                    k.op("dve", lambda e: e.tensor_tensor(out=rlo[hh][:, :], in0=r32[hh][:, :], in1=rhi[hh][:, :],
                                                          op=ALU.subtract), reads=[r32[hh], rhi[hh]], writes=[rlo[hh]])

                def s_part(bi_, blk):
                    r, n = blocks[bi_]
                    qpos = r * L + n * 128
                    hp_ = n > 0
                    pSs = (P[0], P[1]) if blk % 2 == 0 else (P[3], P[4])
                    tm = tmp[blk % 2]
                    pt = pT[blk % 2]
                    qsl = slice(qpos, qpos + 128)
                    psl = slice(qpos - 128, qpos)
                    wdt = 256 if hp_ else 128
                    idn = C["ident"]
                    for hh in range(2):
                        rows = slice(64 * hh, 64 * hh + 64)
                        pS = pSs[hh]
                        mm(k, pS, pS[:, 0:wdt], idn, idn[:, :], rhi[hh], rhi[hh][:, 0:wdt], True, False)
                        mm(k, pS, pS[:, 0:wdt], idn, idn[:, :], rlo[hh], rlo[hh][:, 0:wdt], False, False)
                        mm(k, pS, pS[:, 0:128], ks, ks[rows, qsl], qs, qs[rows, qsl], False, not hp_)
                        if hp_:
                            mm(k, pS, pS[:, 128:256], ks, ks[rows, psl], qs, qs[rows, qsl], False, True)
                    for hh in range(2):
                        c0 = 256 * hh
                        pS = pSs[hh]
                        k.op("act", lambda e: e.activation(out=pt[:, c0:c0 + wdt], in_=pS[:, 0:wdt], func=AF.Exp),
                             reads=[pS], writes=[pt])

                def pv_part(bi_, blk):
                    r, n = blocks[bi_]
                    qpos = r * L + n * 128
                    hp_ = n > 0
                    pU = P[5 + blk % 2]
                    pt = pT[blk % 2]
                    cb = qpos // 128
                    for hh in range(2):
                        orow = slice(64 * hh, 64 * hh + 64)
                        c0 = 256 * hh
                        vsl = slice(64 * hh, 64 * hh + 64)
                        mm(k, pU, pU[orow, 0:128], vs, vs[:, cb, vsl], pt, pt[:, c0:c0 + 128], True, not hp_)
                        if hp_:
                            mm(k, pU, pU[orow, 0:128], vs, vs[:, cb - 1, vsl], pt, pt[:, c0 + 128:c0 + 256], False, True)
                    for hh in range(2):
                        orow = slice(64 * hh, 64 * hh + 64)
                        c0 = 256 * hh
                        mm(k, pU, pU[orow, 128:256], C["ones"], C["ones"][:, 0:64], pt, pt[:, c0:c0 + 128], True, not hp_)
                        if hp_:
                            mm(k, pU, pU[orow, 128:256], C["ones"], C["ones"][:, 0:64], pt, pt[:, c0 + 128:c0 + 256], False, True)
                    ta = r + d * 128 * n
                    accv = acc[:, :, ta:ta + d * 127 + 1:d]
                    puv = pU[:, 0:256].rearrange("p (t q) -> p t q", t=2)
                    if g == 0:
                        k.op("dve", lambda e: e.tensor_copy(accv, puv), reads=[pU], writes=[acc])
                    else:
                        k.op("dve", lambda e: e.tensor_tensor(out=accv, in0=puv, in1=accv, op=ALU.add),
                             reads=[pU, acc], writes=[acc])

                b0 = blkc[0]
                s_part(0, b0)
                for bi_ in range(len(blocks)):
                    if bi_ + 1 < len(blocks):
                        s_part(bi_ + 1, b0 + bi_ + 1)
                    pv_part(bi_, b0 + bi_)
                blkc[0] = b0 + len(blocks)
            for tb in range(8):
                tc = slice(tb * 512, (tb + 1) * 512)
                k.op("dve", lambda e: e.reciprocal(rec[:, :], acc[:, 1, tc]), reads=[acc], writes=[rec])
                k.op("dve", lambda e: e.tensor_tensor(out=ob[:, tc], in0=acc[:, 0, tc], in1=rec[:, :], op=ALU.mult),
                     reads=[acc, rec], writes=[ob])
            k.dma("sp", oT_d, oTv[:, p, :], ob, ob[:, :])


def ssd_prep(k, C, P, hT, w_in, dtb_d, alog_d, dt_tok, acs_tok, acsT_d):
    winv = w_in.t.ap().rearrange("(c p) m -> p c m", p=128)
    k.barrier()
    with ExitStack() as st:
        wdt = k.sbuf("wdt", [128, 8, 32], BF16, st)
        dtb_row = k.sbuf("dtb_row", [128, 32], F32, st)
        nega = k.sbuf("nega_row", [128, 32], F32, st)
        xb = k.sbuf("c0_xb", [128, 32, 32], F32, st)
        ab = k.sbuf("c0_ab", [128, 32, 32], F32, st)
        a_tok = k.sbuf("c0_atok", [128, 32, 32], F32, st)
        acsT_sb = k.sbuf("c0_acsT", [32, S], F32, st)
        k.dma("pool", wdt, wdt[:, :, :], w_in, winv[:, :, DTOFF:DTOFF + 32])
        k.dma("sp", dtb_row, dtb_row[:, :], dtb_d, dtb_d.t.ap().partition_broadcast(128))
        k.dma("sp", nega, nega[:, :], alog_d, alog_d.t.ap().partition_broadcast(128))
        k.op("act", lambda e: e.activation(out=nega[:, :], in_=nega[:, :], func=AF.Exp), reads=[nega], writes=[nega])
        k.op("dve", lambda e: e.tensor_scalar(out=nega[:, :], in0=nega[:, :], scalar1=-1.0, scalar2=None, op0=ALU.mult),
             reads=[nega], writes=[nega])
        for c in range(32):
            pd = P[c % 2]
            for kc in range(8):
                mm(k, pd, pd[:, 0:32], hT, hT[:, kc, c * 128:(c + 1) * 128], wdt, wdt[:, kc, :], kc == 0, kc == 7)
            k.op("dve", lambda e: e.tensor_tensor(out=xb[:, c, :], in0=pd[:, 0:32], in1=dtb_row[:, :], op=ALU.add),
                 reads=[pd, dtb_row], writes=[xb])
        k.op("dve", lambda e: e.scalar_tensor_tensor(out=ab[:, :, :], in0=xb[:, :, :], scalar=-1.0, in1=xb[:, :, :],
                                                     op0=ALU.mult, op1=ALU.max), reads=[xb], writes=[ab])
        k.op("act", lambda e: e.activation(out=ab[:, :, :], in_=ab[:, :, :], func=AF.Exp, scale=-1.0), reads=[ab], writes=[ab])
        k.op("act", lambda e: e.activation(out=ab[:, :, :], in_=ab[:, :, :], func=AF.Ln, bias=1.0), reads=[ab], writes=[ab])
        k.op("dve", lambda e: e.scalar_tensor_tensor(out=dt_tok[:, :, :], in0=xb[:, :, :], scalar=0.0, in1=ab[:, :, :],
                                                     op0=ALU.max, op1=ALU.add), reads=[xb, ab], writes=[dt_tok])
        k.op("dve", lambda e: e.tensor_tensor(out=a_tok[:, :, :], in0=dt_tok[:, :, :], in1=b3(nega[:, :], [128, 32, 32], 1),
                                              op=ALU.mult), reads=[dt_tok, nega], writes=[a_tok])
        for hf in range(2):
            pc = P[2 + hf]
            mm(k, pc, pc[:, :], C["tri"], C["tri"][:, :], a_tok, a_tok[:, 16 * hf:16 * hf + 16, :].rearrange("p c h -> p (c h)"), True, True)
            k.op("dve", lambda e: e.tensor_copy(acs_tok[:, 16 * hf:16 * hf + 16, :].rearrange("p c h -> p (c h)"), pc[:, :]),
                 reads=[pc], writes=[acs_tok])
        for c4 in range(8):
            pc = P[4 + c4 % 2]
            for cc in range(4):
                c = c4 * 4 + cc
                mm(k, pc, pc[0:32, cc * 128:(cc + 1) * 128], a_tok, a_tok[:, c, :], C["tri"], C["tri"][:, :], True, True)
            k.op("act", lambda e: e.activation(out=acsT_sb[:, c4 * 512:(c4 + 1) * 512], in_=pc[0:32, :], func=AF.Copy),
                 reads=[pc], writes=[acsT_sb])
        k.dma("sp", acsT_d, acsT_d.t.ap(), acsT_sb, acsT_sb[:, :])


def ssd_phase(k, C, P, PT, hT, w_in, dt_tok, acs_tok, acsT_d, ynT_d):
    vec = C["vec"]
    winv = w_in.t.ap().rearrange("(c p) m -> p c m", p=128)
    ynv = ynT_d.t.ap().rearrange("(j p) t -> p j t", p=128)
    k.barrier()
    with ExitStack() as st:
        wz = k.sbuf("wz", [128, 8, 512], BF16, st)
        wx = k.sbuf("wx", [128, 8, 512], BF16, st)
        wB = k.sbuf("wB", [128, 8, 128], BF16, st)
        wC = k.sbuf("wC", [128, 8, 128], BF16, st)
        diag = k.sbuf("diag", [128, 24, 128], BF16, st)
        dsk = k.sbuf("dsk", [128, 4, 128], BF16, st)
        rawb = [k.sbuf(f"rawb{i}", [128, 6, 515], BF16, st) for i in range(2)]
        xsT = [k.sbuf(f"xsT{i}", [128, 4, 512], BF16, st) for i in range(2)]
        BT = [k.sbuf(f"BT{i}", [128, 512], BF16, st) for i in range(2)]
        CT = [k.sbuf(f"CT{i}", [128, 512], BF16, st) for i in range(2)]
        sz = [k.sbuf(f"sz{i}", [128, 4, 512], BF16, st) for i in range(2)]
        xdt = [k.sbuf(f"xdt{i}", [128, 512], BF16, st) for i in range(2)]
        xdtd = [k.sbuf(f"xdtd{i}", [128, 512], BF16, st) for i in range(2)]
        Btok = [k.sbuf(f"Btok{i}", [128, 128], BF16, st) for i in range(2)]
        acsrow = [k.sbuf(f"acsrow{i}", [128, 8, 128], F32, st) for i in range(2)]
        Dm = k.sbuf("Dm", [128, 8, 128], F32, st)
        Eb = k.sbuf("Eb", [128, 8, 128], BF16, st)
        MT = [k.sbuf(f"MT{i}", [128, 8, 128], BF16, st) for i in range(2)]
        Eacs = k.sbuf("Eacs", [128, 8, 128], BF16, st)
        Cp = [k.sbuf(f"Cp{i}", [128, 8, 128], BF16, st) for i in range(2)]
        CBm = [k.sbuf(f"CBm{i}", [128, 128], BF16, st) for i in range(2)]
        S32 = k.sbuf("S32", [128, 512], F32, st)
        Sbf = [k.sbuf(f"Sbf{i}", [128, 512], BF16, st) for i in range(2)]
        dd = k.sbuf("dd", [128, 8], F32, st)
        dsb = k.sbuf("dsb", [128, 8], F32, st)
        cdb = [k.sbuf(f"cdb{i}", [128, 8], F32, st) for i in range(2)]
        yv = k.sbuf("yv", [128, 4, 512], F32, st)
        sqy = k.sbuf("sqy", [128, 4, 512], BF16, st)
        lny = k.sbuf("lny", [128, 512], F32, st)
        rsy = k.sbuf("rsy", [128, 512], F32, st)
        ynb = [k.sbuf(f"ynb{i}", [128, 4, 512], BF16, st) for i in range(2)]
        st_ = {"pcnt": 0, "sidx": 0}

        def cjof(g, j):
            return 4 * g + j if j < 4 else (16 + g if j == 4 else 20 + g)

        def pro_part(g, tb, part):
            tc = slice(tb * 512, (tb + 1) * 512)
            rb = rawb[tb % 2]
            if part in (0, 1):
                for j in ((0, 1, 2, 3) if part == 0 else (4, 5)):
                    pj = P[st_["pcnt"] % 2]
                    st_["pcnt"] += 1
                    for c in range(8):
                        if j < 4:
                            mm(k, pj, pj[:, :], wx, wx[:, c, j * 128:(j + 1) * 128], hT, hT[:, c, tc], c == 0, c == 7)
                        else:
                            wb_ = wB if j == 4 else wC
                            mm(k, pj, pj[:, :], wb_, wb_[:, c, :], hT, hT[:, c, tc], c == 0, c == 7)
                    k.op("act", lambda e: e.activation(out=rb[:, j, 3:515], in_=pj[:, :], func=AF.Copy), reads=[pj], writes=[rb])
                if part == 1:
                    if tb == 0:
                        k.op("dve", lambda e: e.memset(rb[:, :, 0:3], 0.0), writes=[rb])
                    else:
                        rp = rawb[(tb - 1) % 2]
                        k.op("dve", lambda e: e.tensor_copy(rb[:, :, 0:3], rp[:, :, 512:515]), reads=[rp], writes=[rb])
            if part in (1, 2):
                szb = sz[tb % 2]
                for j in ((0, 1) if part == 1 else (2, 3)):
                    pj = P[st_["pcnt"] % 2]
                    st_["pcnt"] += 1
                    for c in range(8):
                        mm(k, pj, pj[:, :], wz, wz[:, c, j * 128:(j + 1) * 128], hT, hT[:, c, tc], c == 0, c == 7)
                    k.op("act", lambda e: e.activation(out=szb[:, j, :], in_=pj[:, :], func=AF.Silu), reads=[pj], writes=[szb])
            if part == 2:
                for j in range(6):
                    cj = cjof(g, j)
                    pj = P[st_["pcnt"] % 2]
                    st_["pcnt"] += 1
                    for kk in range(4):
                        mm(k, pj, pj[:, :], diag, diag[:, j * 4 + kk, :], rb, rb[:, j, kk:kk + 512], kk == 0, kk == 3)
                    if j < 4:
                        dstb, dsta = xsT[tb % 2], xsT[tb % 2][:, j, :]
                    elif j == 4:
                        dstb, dsta = BT[tb % 2], BT[tb % 2][:, :]
                    else:
                        dstb, dsta = CT[tb % 2], CT[tb % 2][:, :]
                    k.op("act", lambda e: e.activation(out=dsta, in_=pj[:, :], func=AF.Silu, bias=vec[:, V_CB + cj:V_CB + cj + 1]),
                         reads=[pj, vec], writes=[dstb])

        def ctx(g, c):
            tb, cc = c // 4, c % 4
            return dict(tb=tb, cs=slice(cc * 128, (cc + 1) * 128), xs_=xsT[tb % 2], bt_=BT[tb % 2], ct_=CT[tb % 2],
                        ar=acsrow[c % 2], xd=xdt[c % 2], xdd=xdtd[c % 2], bt=Btok[c % 2], cd_=cdb[c % 2],
                        cbm=CBm[c % 2], mt=MT[c % 2], cp=Cp[c % 2],
                        dts=dt_tok[:, c, 8 * g:8 * g + 8], acs_c=acs_tok[:, c, 8 * g:8 * g + 8])

        def f_dma_pe(g, c):
            x = ctx(g, c)
            ar, cs, xs_, bt_, ct_ = x["ar"], x["cs"], x["xs_"], x["bt_"], x["ct_"]
            src = bass.AP(acsT_d.t, (8 * g) * S + c * 128, [[0, 128], [S, 8], [1, 128]])
            k.dma("sp", ar, ar[:, :, :], acsT_d, src)
            for j in range(4):
                k.op("pe", lambda e: e.transpose(PT[:, j * 128:(j + 1) * 128], xs_[:, j, cs], C["ident"][:, :]),
                     reads=[xs_, C["ident"]], writes=[PT], sig=False)
            k.op("pe", lambda e: e.transpose(PT[:, 512:640], bt_[:, cs], C["ident"][:, :]),
                 reads=[bt_, C["ident"]], writes=[PT])
            mm(k, P[2], P[2][:, 0:128], bt_, bt_[:, cs], ct_, ct_[:, cs], True, True)

        def f_1(g, c):
            x = ctx(g, c)
            ar, xd, bt, cd_ = x["ar"], x["xd"], x["bt"], x["cd_"]
            k.op("dve", lambda e: e.tensor_tensor(out=xd[:, :].rearrange("p (h q) -> p h q", h=8),
                                                  in0=PT[:, 0:512].rearrange("p (h q) -> p h q", h=8),
                                                  in1=b3(x["dts"], [128, 8, 64], 2), op=ALU.mult),
                 reads=[PT, dt_tok], writes=[xd])
            k.op("act", lambda e: e.activation(out=bt[:, :], in_=PT[:, 512:640], func=AF.Copy), reads=[PT], writes=[bt])
            k.op("dve", lambda e: e.tensor_tensor(out=dd[:, :], in0=ar[:, :, 127], in1=x["acs_c"], op=ALU.subtract),
                 reads=[ar, acs_tok], writes=[dd])
            k.op("act", lambda e: e.activation(out=dsb[:, :], in_=dd[:, :], func=AF.Exp), reads=[dd], writes=[dsb])
            k.op("act", lambda e: e.activation(out=cd_[:, :], in_=ar[:, :, 127], func=AF.Exp), reads=[ar], writes=[cd_])
            k.op("dve", lambda e: e.tensor_tensor(out=Dm[:, :, :], in0=ar[:, :, :], in1=b3(x["acs_c"], [128, 8, 128], 2),
                                                  op=ALU.subtract), reads=[ar, acs_tok], writes=[Dm])
            k.op("dve", lambda e: e.tensor_scalar(out=Dm[:, :, :], in0=Dm[:, :, :], scalar1=0.0, scalar2=None, op0=ALU.min),
                 reads=[Dm], writes=[Dm])
            k.op("act", lambda e: e.activation(out=Eb[:, :, :], in_=Dm[:, :, :], func=AF.Exp), reads=[Dm], writes=[Eb])
            if c > 0:
                k.op("act", lambda e: e.activation(out=Eacs[:, :, :], in_=ar[:, :, :], func=AF.Exp), reads=[ar], writes=[Eacs])
            xdd, cbm = x["xdd"], x["cbm"]
            k.op("dve", lambda e: e.tensor_tensor(out=xdd[:, :].rearrange("p (h q) -> p h q", h=8),
                                                  in0=xd[:, :].rearrange("p (h q) -> p h q", h=8),
                                                  in1=b3(dsb[:, :], [128, 8, 64], 2), op=ALU.mult),
                 reads=[xd, dsb], writes=[xdd])
            k.op("dve", lambda e: e.tensor_tensor(out=cbm[:, :], in0=P[2][:, 0:128], in1=C["tri"][:, :], op=ALU.mult),
                 reads=[P[2], C["tri"]], writes=[cbm])

        def f_2(g, c):
            x = ctx(g, c)
            mt, cp, cbm, ct_, cs = x["mt"], x["cp"], x["cbm"], x["ct_"], x["cs"]
            k.op("dve", lambda e: e.tensor_tensor(out=mt[:, :, :], in0=Eb[:, :, :], in1=b3(cbm[:, :], [128, 8, 128], 1),
                                                  op=ALU.mult), reads=[Eb, cbm], writes=[mt])
            if c > 0:
                k.op("dve", lambda e: e.tensor_tensor(out=cp[:, :, :], in0=Eacs[:, :, :], in1=b3(ct_[:, cs], [128, 8, 128], 1),
                                                      op=ALU.mult), reads=[Eacs, ct_], writes=[cp])

        def b_act0(g, c):
            if c > 0:
                sb_prev = Sbf[st_["sidx"] % 2]
                k.op("act", lambda e: e.activation(out=sb_prev[:, :], in_=S32[:, :], func=AF.Copy), reads=[S32], writes=[sb_prev])

        def b_pe(g, c):
            x = ctx(g, c)
            cs, xs_, xd, xdd, bt, mt, cp = x["cs"], x["xs_"], x["xd"], x["xdd"], x["bt"], x["mt"], x["cp"]
            sb_prev = Sbf[st_["sidx"] % 2]
            pY = P[3 + c % 2]
            for hp in range(4):
                ocol = slice(hp * 128, (hp + 1) * 128)
                mm(k, pY, pY[:, ocol], dsk, dsk[:, hp, :], xs_, xs_[:, hp, cs], True, False)
                for hh in range(2):
                    h = 2 * hp + hh
                    orow = slice(64 * hh, 64 * hh + 64)
                    hs = slice(h * 64, (h + 1) * 64)
                    mm(k, pY, pY[orow, ocol], xd, xd[:, hs], mt, mt[:, h, :], False, c == 0)
                    if c > 0:
                        mm(k, pY, pY[orow, ocol], sb_prev, sb_prev[:, hs], cp, cp[:, h, :], False, True)
            if c < 31:
                mm(k, P[5], P[5][:, :], bt, bt[:, :], xdd, xdd[:, :], True, True)

        def b_dve(g, c):
            x = ctx(g, c)
            cs, cd_ = x["cs"], x["cd_"]
            pY = P[3 + c % 2]
            szb = sz[x["tb"] % 2]
            k.op("dve", lambda e: e.tensor_tensor(out=yv[:, :, cs], in0=pY[:, :].rearrange("p (h l) -> p h l", h=4),
                                                  in1=szb[:, :, cs], op=ALU.mult), reads=[pY, szb], writes=[yv])
            if c < 31:
                if c == 0:
                    k.op("dve", lambda e: e.tensor_copy(S32[:, :], P[5][:, :]), reads=[P[5]], writes=[S32])
                else:
                    k.op("dve", lambda e: e.tensor_tensor(out=S32[:, :].rearrange("p (h q) -> p h q", h=8),
                                                          in0=S32[:, :].rearrange("p (h q) -> p h q", h=8),
                                                          in1=b3(cd_[:, :], [128, 8, 64], 2), op=ALU.mult),
                         reads=[S32, cd_], writes=[S32])
                    k.op("dve", lambda e: e.tensor_tensor(out=S32[:, :], in0=S32[:, :], in1=P[5][:, :], op=ALU.add),
                         reads=[S32, P[5]], writes=[S32])
                st_["sidx"] += 1

        def epilogue(g, tb):
            tc = slice(tb * 512, (tb + 1) * 512)
            k.op("act", lambda e: e.activation(out=sqy[:, :, :], in_=yv[:, :, :], func=AF.Square), reads=[yv], writes=[sqy])
            for hp in range(4):
                mm(k, P[6], P[6][:, :], C["ones"], C["ones"][:, :], sqy, sqy[:, hp, :], hp == 0, hp == 3)
            k.op("act", lambda e: e.activation(out=lny[:, :], in_=P[6][:, :], func=AF.Ln, scale=1.0 / 512, bias=C["eps"][:, 0:1]),
                 reads=[P[6], C["eps"]], writes=[lny])
            k.op("act", lambda e: e.activation(out=rsy[:, :], in_=lny[:, :], func=AF.Exp, scale=-0.5), reads=[lny], writes=[rsy])
            yb = ynb[(g * 8 + tb) % 2]
            for hp in range(4):
                ncol_ = V_SSN + 4 * g + hp
                k.op("dve", lambda e: e.scalar_tensor_tensor(out=yb[:, hp, :], in0=yv[:, hp, :], scalar=vec[:, ncol_:ncol_ + 1],
                                                             in1=rsy[:, :], op0=ALU.mult, op1=ALU.mult),
                     reads=[yv, vec, rsy], writes=[yb])
            k.dma("pool", ynT_d, ynv[:, 4 * g:4 * g + 4, tc], yb, yb[:, :, :])

        for g in range(4):
            k.dma("pool", wx, wx[:, :, :], w_in, winv[:, :, XOFF + 512 * g:XOFF + 512 * g + 512])
            k.dma("pool", wB, wB[:, :, :], w_in, winv[:, :, BOFF + 128 * g:BOFF + 128 * g + 128])
            k.dma("pool", wC, wC[:, :, :], w_in, winv[:, :, COFF + 128 * g:COFF + 128 * g + 128])
            k.dma("pool", wz, wz[:, :, :], w_in, winv[:, :, ZOFF + 512 * g:ZOFF + 512 * g + 512])
            for j in range(6):
                cj = cjof(g, j)
                for kk in range(4):
                    col = V_CW + cj * 4 + kk
                    k.op("dve", lambda e: e.tensor_scalar(out=diag[:, j * 4 + kk, :], in0=C["ident"][:, :], scalar1=vec[:, col:col + 1],
                                                          scalar2=None, op0=ALU.mult), reads=[C["ident"], vec], writes=[diag])
            for hp in range(4):
                col = V_DSK + 4 * g + hp
                k.op("dve", lambda e: e.tensor_scalar(out=dsk[:, hp, :], in0=C["ident"][:, :], scalar1=vec[:, col:col + 1],
                                                      scalar2=None, op0=ALU.mult), reads=[C["ident"], vec], writes=[dsk])
            for part in range(3):
                pro_part(g, 0, part)
            f_dma_pe(g, 0)
            f_1(g, 0)
            f_2(g, 0)
            for tb in range(8):
                for cc in range(4):
                    c = tb * 4 + cc
                    b_act0(g, c)
                    if c + 1 < 32:
                        f_dma_pe(g, c + 1)
                    b_pe(g, c)
                    if c + 1 < 32:
                        f_1(g, c + 1)
                    b_dve(g, c)
                    if c + 1 < 32:
                        f_2(g, c + 1)
                    if tb + 1 < 8 and cc < 3:
                        pro_part(g, tb + 1, cc)
                epilogue(g, tb)


def merge_phase(k, C, P, hT, w_in, w_ab, w_sb, w_out, x1T_d, oT_d, ynT_d, x2T_d):
    winv = w_in.t.ap().rearrange("(c p) m -> p c m", p=128)
    wabv = w_ab.t.ap().rearrange("(c p) m -> p c m", p=128)
    wsbv = w_sb.t.ap().rearrange("(c p) m -> p c m", p=128)
    woutv = w_out.t.ap().rearrange("(c p) m -> p c m", p=128)
    x1v = x1T_d.t.ap().rearrange("(c p) t -> p c t", p=128)
    x2v = x2T_d.t.ap().rearrange("(c p) t -> p c t", p=128)
    oTv = oT_d.t.ap().rearrange("(j p) t -> p j t", p=128)
    ynv = ynT_d.t.ap().rearrange("(j p) t -> p j t", p=128)
    k.barrier()
    with ExitStack() as st:
        wout = k.sbuf("wout", [128, 8, 1024], BF16, st)
        x1g = [k.sbuf(f"x1g{i}", [128, 8, 512], F32, st) for i in range(2)]
        yng = [k.sbuf(f"yng{i}", [128, 16, 512], BF16, st) for i in range(1)]
        og = [k.sbuf(f"og{i}", [128, 4, 512], BF16, st) for i in range(2)]
        merged = k.sbuf("merged", [128, 8, 512], BF16, st)
        wab = [k.sbuf(f"wab{i}", [128, 4, 256], BF16, st) for i in range(2)]
        wsb = [k.sbuf(f"wsb{i}", [128, 16, 256], BF16, st) for i in range(2)]
        wga = [k.sbuf(f"wga{i}", [128, 8, 256], BF16, st) for i in range(2)]
        wgs = [k.sbuf(f"wgs{i}", [128, 8, 256], BF16, st) for i in range(2)]
        sga = k.sbuf("sga", [128, 512], F32, st)
        sgs = k.sbuf("sgs", [128, 512], F32, st)
        t1 = k.sbuf("t1", [128, 512], F32, st)
        t2 = k.sbuf("t2", [128, 512], F32, st)
        k.dma("pool", wout, wout[:, :, :], w_out, woutv[:, :, :])

        def loads(G):
            tc = slice(G * 512, (G + 1) * 512)
            k.dma("sp", x1g[G % 2], x1g[G % 2][:, :, :], x1T_d, x1v[:, :, tc])
            k.dma("sp", og[G % 2], og[G % 2][:, :, :], oT_d, oTv[:, :, tc])

        loads(0)
        wcnt = 0
        for G in range(8):
            tc = slice(G * 512, (G + 1) * 512)
            if G + 1 < 8:
                loads(G + 1)
            X = x1g[G % 2]
            yn_ = yng[0]
            k.dma("sp", yn_, yn_[:, :, :], ynT_d, ynv[:, :, tc])
            o_ = og[G % 2]
            for mp in range(4):
                ws = wcnt % 2
                wcnt += 1
                cs = slice(mp * 256, (mp + 1) * 256)
                k.dma("pool", wga[ws], wga[ws][:, :, :], w_in, winv[:, :, GAOFF + mp * 256:GAOFF + (mp + 1) * 256])
                k.dma("pool", wgs[ws], wgs[ws][:, :, :], w_in, winv[:, :, GSOFF + mp * 256:GSOFF + (mp + 1) * 256])
                k.dma("pool", wab[ws], wab[ws][:, :, :], w_ab, wabv[:, :, cs])
                k.dma("pool", wsb[ws], wsb[ws][:, :, :], w_sb, wsbv[:, :, cs])
                for mm_ in range(2):
                    m = mp * 2 + mm_
                    mc = slice(mm_ * 128, (mm_ + 1) * 128)
                    pga, pgs, pa, ps_ = P[0], P[1], P[2], P[3]
                    for c in range(8):
                        mm(k, pga, pga[:, :], wga[ws], wga[ws][:, c, mc], hT, hT[:, c, tc], c == 0, c == 7)
                    for c in range(8):
                        mm(k, pgs, pgs[:, :], wgs[ws], wgs[ws][:, c, mc], hT, hT[:, c, tc], c == 0, c == 7)
                    k.op("act", lambda e: e.activation(out=sga[:, :], in_=pga[:, :], func=AF.Sigmoid), reads=[pga], writes=[sga])
                    k.op("act", lambda e: e.activation(out=sgs[:, :], in_=pgs[:, :], func=AF.Sigmoid), reads=[pgs], writes=[sgs])
                    for c in range(4):
                        mm(k, pa, pa[:, :], wab[ws], wab[ws][:, c, mc], o_, o_[:, c, :], c == 0, c == 3)
                    for c in range(16):
                        mm(k, ps_, ps_[:, :], wsb[ws], wsb[ws][:, c, mc], yn_, yn_[:, c, :], c == 0, c == 15)
                    k.op("dve", lambda e: e.tensor_tensor(out=t1[:, :], in0=sga[:, :], in1=pa[:, :], op=ALU.mult),
                         reads=[sga, pa], writes=[t1])
                    k.op("dve", lambda e: e.tensor_tensor(out=t2[:, :], in0=sgs[:, :], in1=ps_[:, :], op=ALU.mult),
                         reads=[sgs, ps_], writes=[t2])
                    k.op("dve", lambda e: e.tensor_tensor(out=merged[:, m, :], in0=t1[:, :], in1=t2[:, :], op=ALU.add),
                         reads=[t1, t2], writes=[merged])
            for m in range(8):
                po = P[4 + m % 3]
                for c in range(8):
                    mm(k, po, po[:, :], wout, wout[:, c, m * 128:(m + 1) * 128], merged, merged[:, c, :], c == 0, c == 7)
                k.op("dve", lambda e: e.tensor_tensor(out=X[:, m, :], in0=po[:, :], in1=X[:, m, :], op=ALU.add),
                     reads=[po, X], writes=[X])
            k.dma("sp", x2T_d, x2v[:, :, tc], X, X[:, :, :])


def build(debug=False, upto=5):
    nc = bass.Bass("TRN2", target_bir_lowering=False)
    dk = "ExternalOutput" if debug else "Internal"
    with ExitStack() as stack:
        k = KB(nc, stack)
        xT = k.dram("xT", [D, S], F32, "ExternalInput")
        wg1 = k.dram("w_g1", [D, DFF], F32, "ExternalInput")
        wu1 = k.dram("w_u1", [D, DFF], F32, "ExternalInput")
        wd1 = k.dram("w_d1", [DFF, D], F32, "ExternalInput")
        wg2 = k.dram("w_g2", [D, DFF], F32, "ExternalInput")
        wu2 = k.dram("w_u2", [D, DFF], F32, "ExternalInput")
        wd2 = k.dram("w_d2", [DFF, D], F32, "ExternalInput")
        w_in = k.dram("w_in", [D, NCOLS], F32, "ExternalInput")
        w_ab = k.dram("w_ab", [512, D], F32, "ExternalInput")
        w_sb = k.dram("w_sb", [2048, D], F32, "ExternalInput")
        w_out = k.dram("w_out", [D, D], F32, "ExternalInput")
        vecs = k.dram("vecs", [128, NV], F32, "ExternalInput")
        dtb_d = k.dram("dtb", [32], F32, "ExternalInput")
        alog_d = k.dram("alog", [32], F32, "ExternalInput")
        outT = k.dram("outT", [D, S], F32, "ExternalOutput")
        x1T_d = k.dram("x1T", [D, S], F32, dk)
        x2T_d = k.dram("x2T", [D, S], F32, dk)
        oT_d = k.dram("oT", [512, S], BF16, dk)
        ynT_d = k.dram("ynT", [2048, S], BF16, dk)
        acsT_d = k.dram("acsT", [32, S], F32, dk)
        hT_d = k.dram("hT_dbg", [D, S], BF16, dk) if debug else None

        P = [k.psum(f"ps{i}", [128, 512], F32) for i in range(7)]
        PT = k.psum("pst", [128, 1024], BF16)

        C = {}
        vec = k.sbuf("vec", [128, NV], F32)
        C["vec"] = vec
        k.dma("sp", vec, vec[:, :], vecs, vecs.t.ap())
        ones = k.sbuf("ones_bf", [128, 128], BF16)
        k.op("dve", lambda e: e.memset(ones[:, :], 1.0), writes=[ones])
        C["ones"] = ones
        bd = k.sbuf("bd_bf", [128, 128], BF16)
        k.op("dve", lambda e: e.memset(bd[:, :], 0.0), writes=[bd])
        k.op("dve", lambda e: e.memset(bd[0:64, 0:64], 1.0), writes=[bd])
        k.op("dve", lambda e: e.memset(bd[64:128, 64:128], 1.0), writes=[bd])
        C["bd"] = bd
        epsb = k.sbuf("eps_c", [128, 1], F32)
        k.op("dve", lambda e: e.memset(epsb[:, :], EPS), writes=[epsb])
        C["eps"] = epsb
        identf = k.sbuf("identf", [128, 128], F32)
        k.op("pool", lambda e: e.memset(identf[:, :], 0.0), writes=[identf])
        k.op("pool", lambda e: e.affine_select(out=identf[:, :], in_=identf[:, :], pattern=[[-1, 128]],
                                               compare_op=ALU.not_equal, fill=1.0, base=0, channel_multiplier=1),
             reads=[identf], writes=[identf])
        ident = k.sbuf("ident_bf", [128, 128], BF16)
        k.op("dve", lambda e: e.tensor_copy(ident[:, :], identf[:, :]), reads=[identf], writes=[ident])
        C["ident"] = ident
        tri = k.sbuf("tri_f", [128, 128], F32)
        k.op("pool", lambda e: e.memset(tri[:, :], 1.0), writes=[tri])
        k.op("pool", lambda e: e.affine_select(out=tri[:, :], in_=tri[:, :], pattern=[[1, 128]],
                                               compare_op=ALU.is_ge, fill=0.0, base=0, channel_multiplier=-1),
             reads=[tri], writes=[tri])
        C["tri"] = tri
        r2i = k.sbuf("r2i", [128, 256], I32)
        R2 = k.sbuf("R2", [128, 256], F32)
        k.op("pool", lambda e: e.iota(r2i[:, 0:128], pattern=[[1, 128]], base=0, channel_multiplier=-1), writes=[r2i])
        k.op("pool", lambda e: e.iota(r2i[:, 128:256], pattern=[[1, 128]], base=128, channel_multiplier=-1),
             reads=[r2i], writes=[r2i])
        k.op("dve", lambda e: e.tensor_copy(R2[:, :], r2i[:, :]), reads=[r2i], writes=[R2])
        k.op("pool", lambda e: e.affine_select(out=R2[:, 0:128], in_=R2[:, 0:128], pattern=[[1, 128]],
                                               compare_op=ALU.is_ge, fill=BIG, base=0, channel_multiplier=-1),
             reads=[R2], writes=[R2])
        k.op("pool", lambda e: e.affine_select(out=R2[:, 128:256], in_=R2[:, 128:256], pattern=[[-1, 128]],
                                               compare_op=ALU.is_ge, fill=BIG, base=0, channel_multiplier=1),
             reads=[R2], writes=[R2])
        C["R2"] = R2
        k.op("dve", lambda e: e.tensor_scalar(out=vec[:, V_GQ:V_GQ + 1], in0=vec[:, V_GQ:V_GQ + 1], scalar1=0.125, scalar2=None,
                                              op0=ALU.mult), reads=[vec], writes=[vec])

        hT = k.sbuf("hT", [128, 8, S], BF16)

        last = x1T_d
        ffn_phase(k, C, P, "f1", xT, x1T_d, wg1, wu1, wd1, V_N1, mix=(V_NM, hT))
        if debug:
            hv = hT_d.t.ap().rearrange("(c p) t -> p c t", p=128)
            k.dma("sp", hT_d, hv, hT, hT[:, :, :])
        outs = [x1T_d]
        if upto >= 2:
            attn_phase(k, C, P, hT, w_in, oT_d)
            outs.append(oT_d)
        if upto >= 3:
            with ExitStack() as sst:
                dt_tok = k.sbuf("dt_tok", [128, 32, 32], F32, sst)
                acs_tok = k.sbuf("acs_tok", [128, 32, 32], F32, sst)
                ssd_prep(k, C, P, hT, w_in, dtb_d, alog_d, dt_tok, acs_tok, acsT_d)
                ssd_phase(k, C, P, PT, hT, w_in, dt_tok, acs_tok, acsT_d, ynT_d)
            outs.append(ynT_d)
        if upto >= 4:
            merge_phase(k, C, P, hT, w_in, w_ab, w_sb, w_out, x1T_d, oT_d, ynT_d, x2T_d)
            outs.append(x2T_d)
        if upto >= 5:
            ffn_phase(k, C, P, "f2", x2T_d, outT, wg2, wu2, wd2, V_N2, mix=None)
            outs.append(outT)
        if debug:
            outs.append(hT_d)
        k.finish("sp", outs)
        build.stats = (k.n_inst, k.n_wait, dict(k.ecnt), len(k.sems))
    return nc


def prep_inputs(inputs):
    f = lambda a: np.ascontiguousarray(np.asarray(a, dtype=np.float32))
    x = f(inputs["x"])
    vec = np.zeros((128, NV), np.float32)
    pc = lambda v: f(v).reshape(-1, 128).T
    vec[:, V_N1:V_N1 + 8] = pc(inputs["ffn1_norm"][0])
    vec[:, V_NM:V_NM + 8] = pc(inputs["mix_norm"][0])
    vec[:, V_N2:V_N2 + 8] = pc(inputs["ffn2_norm"][0])
    vec[:, V_GQ] = np.tile(f(inputs["q_norm"][0]), 2)
    vec[:, V_GK] = np.tile(f(inputs["k_norm"][0]), 2)
    cw = f(inputs["conv_w"][0])
    vec[:, V_CW:V_CW + 96] = cw.reshape(4, 24, 128).transpose(2, 1, 0).reshape(128, 96)
    vec[:, V_CB:V_CB + 24] = pc(inputs["conv_b"][0])
    vec[:, V_DSK:V_DSK + 16] = pc(np.repeat(f(inputs["d_skip"][0]), 64))
    vec[:, V_SSN:V_SSN + 16] = pc(inputs["ssd_norm"][0])
    shared = {
        "w_g1": f(inputs["ffn1_w_gate"][0]), "w_u1": f(inputs["ffn1_w_up"][0]), "w_d1": f(inputs["ffn1_w_down"][0]),
        "w_g2": f(inputs["ffn2_w_gate"][0]), "w_u2": f(inputs["ffn2_w_up"][0]), "w_d2": f(inputs["ffn2_w_down"][0]),
        "w_in": f(inputs["w_in"][0]), "w_ab": f(inputs["w_attn_branch"][0]), "w_sb": f(inputs["w_ssd_branch"][0]),
        "w_out": f(inputs["w_out"][0]), "vecs": vec, "dtb": f(inputs["dt_bias"][0]), "alog": f(inputs["a_log"][0]),
    }
    in_maps = []
    for b in range(x.shape[0]):
        m = dict(shared)
        m["xT"] = np.ascontiguousarray(x[b].T)
        in_maps.append(m)
    return in_maps


_NC = None


def kernel(**inputs):
    global _NC
    if _NC is None:
        _NC = build()
    in_maps = prep_inputs(inputs)
    res = run_bass_kernel_spmd(_NC, in_maps, core_ids=list(range(8)))
    out = np.stack([np.ascontiguousarray(np.asarray(r["outT"]).T) for r in res.results], axis=0)
    return out.astype(np.float32)
```

```python
import math
from contextlib import ExitStack
import numpy as np
import concourse.bass as bass
import concourse.mybir as mybir
from concourse.bass_utils import run_bass_kernel_spmd

F32 = mybir.dt.float32
BF16 = mybir.dt.bfloat16
I32 = mybir.dt.int32
AF = mybir.ActivationFunctionType
ALU = mybir.AluOpType

S = 4096
D = 1024
DFF = 2816
NCOLS = 11808
QOFF, KOFF, VOFF, ZOFF, XOFF, BOFF, COFF, DTOFF, GAOFF, GSOFF = 0, 1536, 3072, 4608, 6656, 8704, 9216, 9728, 9760, 10784
EPS = 1e-6
V_N1, V_NM, V_N2, V_GQ, V_GK, V_CW, V_CB, V_DSK, V_SSN, NV = 0, 8, 16, 24, 25, 26, 122, 146, 162, 178
BIG = 1.0e6
ATT = [4, 3, 9]


class Buf:
    __slots__ = ("name", "t", "last_w", "reads", "dsem", "dcnt", "is_psum")

    def __init__(self, name, t):
        self.is_psum = False
        self.name = name
        self.t = t
        self.last_w = None
        self.reads = {}
        self.dsem = None
        self.dcnt = 0

    def __getitem__(self, idx):
        return self.t[idx]


class KB:
    def __init__(self, nc, stack):
        self.nc = nc
        self.stack = stack
        self.engs = {"pe": nc.tensor, "act": nc.scalar, "dve": nc.vector, "pool": nc.gpsimd, "sp": nc.sync}
        self.sems = {}
        self.ecnt = {}
        for e in ("pe", "act", "dve", "pool"):
            self.sems["E" + e] = stack.enter_context(nc.semaphore("sem_" + e))
            self.ecnt[e] = 0
        self.seen = {e: {} for e in self.engs}
        self.n_inst = 0
        self.n_wait = 0
        self.dcur = {}

    def sbuf(self, name, shape, dtype, st=None):
        t = (st or self.stack).enter_context(self.nc.sbuf_tensor(name, list(shape), dtype))
        return Buf(name, t)

    def psum(self, name, shape, dtype):
        t = self.stack.enter_context(self.nc.psum_tensor(name, list(shape), dtype))
        b = Buf(name, t)
        b.is_psum = True
        return b

    def dram(self, name, shape, dtype, kind="Internal"):
        return Buf(name, self.nc.dram_tensor(name, list(shape), dtype, kind=kind))

    def _wait(self, eng, ev):
        if ev is None:
            return
        key, val = ev
        if self.seen[eng].get(key, 0) >= val:
            return
        self.engs[eng].wait_ge(self.sems[key], val)
        self.seen[eng][key] = val
        self.n_wait += 1

    def _deps(self, eng, reads, writes):
        own = "E" + eng
        for b in reads:
            ev = b.last_w
            if ev is not None and not (ev[0] == own and eng == "pe"):
                self._wait(eng, ev)
        for b in writes:
            ev = b.last_w
            if ev is not None and not (ev[0] == own and eng == "pe"):
                self._wait(eng, ev)
            for key, val in b.reads.items():
                if key != own or eng != "pe":
                    self._wait(eng, (key, val))

    def _commit(self, ev, reads, writes):
        key, val = ev
        for b in reads:
            if b.reads.get(key, 0) < val:
                b.reads[key] = val
        for b in writes:
            b.last_w = ev
            b.reads = {}

    def barrier(self):
        for e in self.engs:
            for key in self.sems:
                if key.startswith("E"):
                    val = self.ecnt[key[1:]]
                    if key == "E" + e:
                        continue
                else:
                    val = self.dcur.get(key, 0)
                if val > 0:
                    self._wait(e, (key, val))

    def op(self, eng, fn, reads=(), writes=(), sig=True):
        if eng != "pe":
            pr = [b for b in reads if b.is_psum]
            if pr:
                writes = list(writes) + [b for b in pr if b not in writes]
        self._deps(eng, reads, writes)
        inst = fn(self.engs[eng])
        self.n_inst += 1
        key = "E" + eng
        if sig:
            self.ecnt[eng] += 1
            inst.then_inc(self.sems[key], 1)
            ev = (key, self.ecnt[eng])
        else:
            ev = (key, self.ecnt[eng] + 1)
        self._commit(ev, reads, writes)

    def dma(self, q, out_buf, out_ap, in_buf, in_ap):
        self._deps(q, [in_buf], [out_buf])
        if out_buf.dsem is None:
            key = "D" + out_buf.name
            self.sems[key] = self.stack.enter_context(self.nc.semaphore("dsem_" + out_buf.name))
            out_buf.dsem = key
        out_buf.dcnt += 16
        self.dcur[out_buf.dsem] = out_buf.dcnt
        inst = self.engs[q].dma_start(out=out_ap, in_=in_ap)
        inst.then_inc(self.sems[out_buf.dsem], 16)
        self.n_inst += 1
        self._commit((out_buf.dsem, out_buf.dcnt), [in_buf], [out_buf])

    def finish(self, eng, bufs):
        for b in bufs:
            self._wait(eng, b.last_w)


def mm(k, out_buf, out_ap, l_buf, l_ap, r_buf, r_ap, start, stop):
    k.op("pe", lambda e: e.matmul(out_ap, l_ap, r_ap, start=start, stop=stop),
         reads=[l_buf, r_buf], writes=[out_buf], sig=stop)


def b3(ap, shape, axis):
    return ap.unsqueeze(axis).broadcast_to(shape)


def rmsnorm_fm(k, C, X, nchunk, gcol, out_buf, out_fn, sq, lnv, rstd, pss, dnorm):
    k.op("act", lambda e: e.activation(out=sq[:, 0:nchunk, :], in_=X[:, 0:nchunk, :], func=AF.Square),
         reads=[X], writes=[sq])
    for c in range(nchunk):
        mm(k, pss, pss[:, :], C["ones"], C["ones"][:, :], sq, sq[:, c, :], c == 0, c == nchunk - 1)
    k.op("act", lambda e: e.activation(out=lnv[:, :], in_=pss[:, :], func=AF.Ln, scale=1.0 / dnorm, bias=C["eps"][:, 0:1]),
         reads=[pss, C["eps"]], writes=[lnv])
    k.op("act", lambda e: e.activation(out=rstd[:, :], in_=lnv[:, :], func=AF.Exp, scale=-0.5),
         reads=[lnv], writes=[rstd])
    vec = C["vec"]
    for c in range(nchunk):
        k.op("dve", lambda e: e.scalar_tensor_tensor(out=out_fn(c), in0=X[:, c, :], scalar=vec[:, gcol + c:gcol + c + 1],
                                                     in1=rstd[:, :], op0=ALU.mult, op1=ALU.mult),
             reads=[X, vec, rstd], writes=[out_buf])


def norm_a(k, X, out_buf, out_fn):
    for c in range(8):
        k.op("act", lambda e: e.activation(out=out_fn(c), in_=X[:, c, :], func=AF.Square), reads=[X], writes=[out_buf])


def norm_b(k, C, X, gcol, out_buf, out_fn, lnv, rstd, pss):
    for c in range(8):
        mm(k, pss, pss[:, :], C["ones"], C["ones"][:, :], out_buf, out_fn(c), c == 0, c == 7)
    k.op("act", lambda e: e.activation(out=lnv[:, :], in_=pss[:, :], func=AF.Ln, scale=1.0 / 1024, bias=C["eps"][:, 0:1]),
         reads=[pss, C["eps"]], writes=[lnv])
    k.op("act", lambda e: e.activation(out=rstd[:, :], in_=lnv[:, :], func=AF.Exp, scale=-0.5),
         reads=[lnv], writes=[rstd])
    vec = C["vec"]
    for c in range(8):
        k.op("dve", lambda e: e.scalar_tensor_tensor(out=out_fn(c), in0=X[:, c, :], scalar=vec[:, gcol + c:gcol + c + 1],
                                                     in1=rstd[:, :], op0=ALU.mult, op1=ALU.mult),
             reads=[X, vec, rstd], writes=[out_buf])


def ffn_phase(k, C, P, name, src, dst, wg, wu, wd, ncol, mix=None):
    k.barrier()
    with ExitStack() as st:
        xg = [k.sbuf(f"{name}_xg{i}", [128, 8, 512], F32, st) for i in range(2)]
        h1 = [k.sbuf(f"{name}_h1{i}", [128, 8, 512], BF16, st) for i in range(2)]
        act = k.sbuf(name + "_act", [128, 22, 512], BF16, st)
        gw = [k.sbuf(f"{name}_gw{i}", [128, 8, 512], BF16, st) for i in range(2)]
        uw = [k.sbuf(f"{name}_uw{i}", [128, 8, 512], BF16, st) for i in range(2)]
        dw = [k.sbuf(f"{name}_dw{i}", [128, 22, 256], BF16, st) for i in range(2)]
        sg = [k.sbuf(f"{name}_sg{i}", [128, 512], BF16, st) for i in range(2)]
        lnv = [k.sbuf(f"{name}_lnv{i}", [128, 512], F32, st) for i in range(2)]
        rstd = [k.sbuf(f"{name}_rstd{i}", [128, 512], F32, st) for i in range(2)]
        srcv = src.t.ap().rearrange("(c p) t -> p c t", p=128)
        dstv = dst.t.ap().rearrange("(c p) t -> p c t", p=128)
        wgv = wg.t.ap().rearrange("(c p) m -> p c m", p=128)
        wuv = wu.t.ap().rearrange("(c p) m -> p c m", p=128)
        wdv = wd.t.ap().rearrange("(c p) m -> p c m", p=128)
        k.dma("sp", xg[0], xg[0][:, :, :], src, srcv[:, :, 0:512])
        H0 = h1[0]
        norm_a(k, xg[0], H0, lambda c: H0[:, c, :])
        norm_b(k, C, xg[0], ncol, H0, lambda c: H0[:, c, :], lnv[0], rstd[0], P[6])
        wcnt = 0
        dcnt = 0
        pend_mix = None
        for G in range(8):
            X = xg[G % 2]
            H = h1[G % 2]
            tc = slice(G * 512, (G + 1) * 512)
            if G + 1 < 8:
                Xn = xg[(G + 1) % 2]
                Hn = h1[(G + 1) % 2]
            for jb in range(6):
                cols = 512 if jb < 5 else 256
                ws = wcnt % 2
                wcnt += 1
                k.dma("pool", gw[ws], gw[ws][:, :, 0:cols], wg, wgv[:, :, jb * 512:jb * 512 + cols])
                k.dma("pool", uw[ws], uw[ws][:, :, 0:cols], wu, wuv[:, :, jb * 512:jb * 512 + cols])
                if jb == 4 and G + 1 < 8:
                    norm_a(k, Xn, Hn, lambda c: Hn[:, c, :])
                for jj in range(cols // 128):
                    j = jb * 4 + jj
                    pg, pu = P[(j % 2) * 2], P[(j % 2) * 2 + 1]
                    ms = slice(jj * 128, (jj + 1) * 128)
                    for c in range(8):
                        mm(k, pg, pg[:, :], gw[ws], gw[ws][:, c, ms], H, H[:, c, :], c == 0, c == 7)
                    for c in range(8):
                        mm(k, pu, pu[:, :], uw[ws], uw[ws][:, c, ms], H, H[:, c, :], c == 0, c == 7)
                    sgb = sg[j % 2]
                    k.op("act", lambda e: e.activation(out=sgb[:, :], in_=pg[:, :], func=AF.Silu), reads=[pg], writes=[sgb])
                    k.op("dve", lambda e: e.tensor_tensor(out=act[:, j, :], in0=sgb[:, :], in1=pu[:, :], op=ALU.mult),
                         reads=[sgb, pu], writes=[act])
                if jb == 0:
                    if pend_mix is not None:
                        pend_mix()
                        pend_mix = None
                    if G + 1 < 8:
                        k.dma("sp", Xn, Xn[:, :, :], src, srcv[:, :, (G + 1) * 512:(G + 2) * 512])
            if G + 1 < 8:
                norm_b(k, C, Xn, ncol, Hn, lambda c: Hn[:, c, :], lnv[0], rstd[0], P[6])
            for mp in range(4):
                ds_ = dcnt % 2
                dcnt += 1
                k.dma("pool", dw[ds_], dw[ds_][:, :, :], wd, wdv[:, :, mp * 256:(mp + 1) * 256])
                for mm_ in range(2):
                    m = mp * 2 + mm_
                    pb = P[4 + m % 2]
                    for kc in range(22):
                        mm(k, pb, pb[:, :], dw[ds_], dw[ds_][:, kc, mm_ * 128:(mm_ + 1) * 128], act, act[:, kc, :], kc == 0, kc == 21)
                    k.op("dve", lambda e: e.scalar_tensor_tensor(out=X[:, m, :], in0=pb[:, :], scalar=0.5, in1=X[:, m, :],
                                                                 op0=ALU.mult, op1=ALU.add), reads=[pb, X], writes=[X])
            k.dma("sp", dst, dstv[:, :, tc], X, X[:, :, :])
            if mix is not None:
                mcol, hT = mix
                norm_a(k, X, hT, lambda c, tc=tc: hT[:, c, tc])
                pend_mix = (lambda X=X, tc=tc: norm_b(k, C, X, mcol, hT, lambda c: hT[:, c, tc], lnv[1], rstd[1], P[6]))
        if pend_mix is not None:
            pend_mix()


def attn_phase(k, C, P, hT, w_in, oT_d):
    vec = C["vec"]
    winv = w_in.t.ap().rearrange("(c p) m -> p c m", p=128)
    oTv = oT_d.t.ap().rearrange("(j p) t -> p j t", p=128)
    k.barrier()
    with ExitStack() as st:
        acc = k.sbuf("acc", [128, 2, S], F32, st)
        qs = k.sbuf("qs", [128, S], BF16, st)
        ks = k.sbuf("ks", [128, S], BF16, st)
        vs = k.sbuf("vs", [128, 32, 128], BF16, st)
        wqs = [k.sbuf(f"wq{i}", [128, 8, 128], BF16, st) for i in range(2)]
        wks = [k.sbuf(f"wk{i}", [128, 8, 128], BF16, st) for i in range(2)]
        wvs = [k.sbuf(f"wv{i}", [128, 8, 128], BF16, st) for i in range(2)]
        tmp = [k.sbuf(f"a_tmp{i}", [128, 512], F32, st) for i in range(2)]
        pT = [k.sbuf(f"a_pT{i}", [128, 512], BF16, st) for i in range(2)]
        sqa = [k.sbuf(f"a_sq{i}", [128, 512], BF16, st) for i in range(2)]
        lnq = k.sbuf("a_ln", [128, 512], F32, st)
        rsq = [k.sbuf(f"a_rs{i}", [128, 512], F32, st) for i in range(2)]
        rec = k.sbuf("a_rec", [128, 512], F32, st)
        ob = k.sbuf("a_ob", [128, S], BF16, st)
        hgs = [k.sbuf(f"a_hg{i}", [128, 8, 512], BF16, st) for i in range(2)]
        r32 = [k.sbuf(f"a_r32{i}", [128, 256], F32, st) for i in range(2)]
        rhis = [[k.sbuf(f"a_rhi{j}{i}", [128, 256], BF16, st) for i in range(2)] for j in range(2)]
        rlos = [[k.sbuf(f"a_rlo{j}{i}", [128, 256], BF16, st) for i in range(2)] for j in range(2)]
        blkc = [0]
        it = 0
        for p in range(ATT[0]):
            for g in range(ATT[1]):
                d = (1, 4, 16)[g]
                L = S // d
                hA = 8 * g + 2 * p
                wq, wk, wv = wqs[it % 2], wks[it % 2], wvs[it % 2]
                it += 1
                k.dma("pool", wq, wq[:, :, :], w_in, winv[:, :, QOFF + hA * 64:QOFF + hA * 64 + 128])
                k.dma("pool", wk, wk[:, :, :], w_in, winv[:, :, KOFF + hA * 64:KOFF + hA * 64 + 128])
                k.dma("pool", wv, wv[:, :, :], w_in, winv[:, :, VOFF + hA * 64:VOFF + hA * 64 + 128])
                NB = min(512, L)
                items = []
                gsrc = {}
                gcnt = 0
                for r in range(d):
                    for i0 in range(0, L, NB):
                        items.append((wq, qs, V_GQ, r, i0))
                        items.append((wk, ks, V_GK, r, i0))
                        if d > 1:
                            gsrc[(r, i0)] = hgs[gcnt % 2]
                            gcnt += 1

                def gather(r, i0):
                    hg = gsrc[(r, i0)]
                    t0 = r + d * i0
                    tsl = slice(t0, t0 + d * (NB - 1) + 1, d)
                    k.op("dve", lambda e: e.tensor_copy(hg[:, :, 0:NB], hT[:, :, tsl]), reads=[hT], writes=[hg])

                def proj_mm(ii):
                    w, dstb, gcol, r, i0 = items[ii]
                    t0 = r + d * i0
                    tsl = slice(t0, t0 + d * (NB - 1) + 1, d)
                    pq = P[(0, 1, 3, 4)[ii % 4]]
                    if d > 1:
                        if ii % 2 == 0:
                            if ii == 0:
                                gather(r, i0)
                            if ii + 2 < len(items):
                                gather(items[ii + 2][3], items[ii + 2][4])
                        hg = gsrc[(r, i0)]
                        for c in range(8):
                            mm(k, pq, pq[:, 0:NB], w, w[:, c, :], hg, hg[:, c, 0:NB], c == 0, c == 7)
                    else:
                        for c in range(8):
                            mm(k, pq, pq[:, 0:NB], w, w[:, c, :], hT, hT[:, c, tsl], c == 0, c == 7)
                    sq_ = sqa[ii % 2]
                    k.op("act", lambda e: e.activation(out=sq_[:, 0:NB], in_=pq[:, 0:NB], func=AF.Square),
                         reads=[pq], writes=[sq_])

                def proj_norm(ii):
                    w, dstb, gcol, r, i0 = items[ii]
                    pos0 = r * L + i0
                    pq = P[(0, 1, 3, 4)[ii % 4]]
                    sq_ = sqa[ii % 2]
                    rs_ = rsq[ii % 2]
                    mm(k, P[2], P[2][:, 0:NB], C["bd"], C["bd"][:, :], sq_, sq_[:, 0:NB], True, True)
                    k.op("act", lambda e: e.activation(out=lnq[:, 0:NB], in_=P[2][:, 0:NB], func=AF.Ln, scale=1.0 / 64,
                                                       bias=C["eps"][:, 0:1]), reads=[P[2], C["eps"]], writes=[lnq])
                    k.op("act", lambda e: e.activation(out=rs_[:, 0:NB], in_=lnq[:, 0:NB], func=AF.Exp, scale=-0.5),
                         reads=[lnq], writes=[rs_])
                    k.op("dve", lambda e: e.scalar_tensor_tensor(out=dstb[:, pos0:pos0 + NB], in0=pq[:, 0:NB],
                                                                 scalar=vec[:, gcol:gcol + 1], in1=rs_[:, 0:NB],
                                                                 op0=ALU.mult, op1=ALU.mult),
                         reads=[pq, vec, rs_], writes=[dstb])

                proj_mm(0)
                for ii in range(len(items)):
                    if ii + 1 < len(items):
                        proj_mm(ii + 1)
                    proj_norm(ii)
                vcnt = 0
                for r in range(d):
                    for b in range(L // 128):
                        pv = P[(3, 4, 0, 1)[vcnt % 4]]
                        vcnt += 1
                        tb0 = r + d * b * 128
                        tsb = slice(tb0, tb0 + d * 127 + 1, d)
                        for c in range(8):
                            mm(k, pv, pv[:, 0:128], hT, hT[:, c, tsb], wv, wv[:, c, :], c == 0, c == 7)
                        bi = (r * L) // 128 + b
                        k.op("act", lambda e: e.activation(out=vs[:, bi, :], in_=pv[:, 0:128], func=AF.Copy),
                             reads=[pv], writes=[vs])
                slopes = [2.0 ** (-8.0 * (hA + hh + 1) / 24.0) for hh in range(2)]
                blocks = [(r, n) for r in range(d) for n in range(L // 128)]
                rhi, rlo = rhis[it % 2], rlos[it % 2]
                for hh in range(2):
                    sc = -slopes[hh] * d
                    k.op("dve", lambda e: e.tensor_scalar(out=r32[hh][:, :], in0=C["R2"][:, :], scalar1=sc, scalar2=None,
                                                          op0=ALU.mult), reads=[C["R2"]], writes=[r32[hh]])
                    k.op("dve", lambda e: e.tensor_copy(rhi[hh][:, :], r32[hh][:, :]), reads=[r32[hh]], writes=[rhi[hh]])
                    k.op("dve", lambda e: e.tensor_tensor(out=rlo[hh][:, :], in0=r32[hh][:, :], in1=rhi[hh][:, :],
                                                          op=ALU.subtract), reads=[r32[hh], rhi[hh]], writes=[rlo[hh]])

                def s_part(bi_, blk):
                    r, n = blocks[bi_]
                    qpos = r * L + n * 128
                    hp_ = n > 0
                    pSs = (P[0], P[1]) if blk % 2 == 0 else (P[3], P[4])
                    tm = tmp[blk % 2]
                    pt = pT[blk % 2]
                    qsl = slice(qpos, qpos + 128)
                    psl = slice(qpos - 128, qpos)
                    wdt = 256 if hp_ else 128
                    idn = C["ident"]
                    for hh in range(2):
                        rows = slice(64 * hh, 64 * hh + 64)
                        pS = pSs[hh]
                        mm(k, pS, pS[:, 0:wdt], idn, idn[:, :], rhi[hh], rhi[hh][:, 0:wdt], True, False)
                        mm(k, pS, pS[:, 0:wdt], idn, idn[:, :], rlo[hh], rlo[hh][:, 0:wdt], False, False)
                        mm(k, pS, pS[:, 0:128], ks, ks[rows, qsl], qs, qs[rows, qsl], False, not hp_)
                        if hp_:
                            mm(k, pS, pS[:, 128:256], ks, ks[rows, psl], qs, qs[rows, qsl], False, True)
                    for hh in range(2):
                        c0 = 256 * hh
                        pS = pSs[hh]
                        k.op("act", lambda e: e.activation(out=pt[:, c0:c0 + wdt], in_=pS[:, 0:wdt], func=AF.Exp),
                             reads=[pS], writes=[pt])

                def pv_part(bi_, blk):
                    r, n = blocks[bi_]
                    qpos = r * L + n * 128
                    hp_ = n > 0
                    pU = P[5 + blk % 2]
                    pt = pT[blk % 2]
                    cb = qpos // 128
                    for hh in range(2):
                        orow = slice(64 * hh, 64 * hh + 64)
                        c0 = 256 * hh
                        vsl = slice(64 * hh, 64 * hh + 64)
                        mm(k, pU, pU[orow, 0:128], vs, vs[:, cb, vsl], pt, pt[:, c0:c0 + 128], True, not hp_)
                        if hp_:
                            mm(k, pU, pU[orow, 0:128], vs, vs[:, cb - 1, vsl], pt, pt[:, c0 + 128:c0 + 256], False, True)
                    for hh in range(2):
                        orow = slice(64 * hh, 64 * hh + 64)
                        c0 = 256 * hh
                        mm(k, pU, pU[orow, 128:256], C["ones"], C["ones"][:, 0:64], pt, pt[:, c0:c0 + 128], True, not hp_)
                        if hp_:
                            mm(k, pU, pU[orow, 128:256], C["ones"], C["ones"][:, 0:64], pt, pt[:, c0 + 128:c0 + 256], False, True)
                    ta = r + d * 128 * n
                    accv = acc[:, :, ta:ta + d * 127 + 1:d]
                    puv = pU[:, 0:256].rearrange("p (t q) -> p t q", t=2)
                    if g == 0:
                        k.op("dve", lambda e: e.tensor_copy(accv, puv), reads=[pU], writes=[acc])
                    else:
                        k.op("dve", lambda e: e.tensor_tensor(out=accv, in0=puv, in1=accv, op=ALU.add),
                             reads=[pU, acc], writes=[acc])

                b0 = blkc[0]
                s_part(0, b0)
                for bi_ in range(len(blocks)):
                    if bi_ + 1 < len(blocks):
                        s_part(bi_ + 1, b0 + bi_ + 1)
                    pv_part(bi_, b0 + bi_)
                blkc[0] = b0 + len(blocks)
            for tb in range(8):
                tc = slice(tb * 512, (tb + 1) * 512)
                k.op("dve", lambda e: e.reciprocal(rec[:, :], acc[:, 1, tc]), reads=[acc], writes=[rec])
                k.op("dve", lambda e: e.tensor_tensor(out=ob[:, tc], in0=acc[:, 0, tc], in1=rec[:, :], op=ALU.mult),
                     reads=[acc, rec], writes=[ob])
            k.dma("sp", oT_d, oTv[:, p, :], ob, ob[:, :])


def ssd_prep(k, C, P, hT, w_in, dtb_d, alog_d, dt_tok, acs_tok, acsT_d):
    winv = w_in.t.ap().rearrange("(c p) m -> p c m", p=128)
    k.barrier()
    with ExitStack() as st:
        wdt = k.sbuf("wdt", [128, 8, 32], BF16, st)
        dtb_row = k.sbuf("dtb_row", [128, 32], F32, st)
        nega = k.sbuf("nega_row", [128, 32], F32, st)
        xb = k.sbuf("c0_xb", [128, 32, 32], F32, st)
        ab = k.sbuf("c0_ab", [128, 32, 32], F32, st)
        a_tok = k.sbuf("c0_atok", [128, 32, 32], F32, st)
        acsT_sb = k.sbuf("c0_acsT", [32, S], F32, st)
        k.dma("pool", wdt, wdt[:, :, :], w_in, winv[:, :, DTOFF:DTOFF + 32])
        k.dma("sp", dtb_row, dtb_row[:, :], dtb_d, dtb_d.t.ap().partition_broadcast(128))
        k.dma("sp", nega, nega[:, :], alog_d, alog_d.t.ap().partition_broadcast(128))
        k.op("act", lambda e: e.activation(out=nega[:, :], in_=nega[:, :], func=AF.Exp), reads=[nega], writes=[nega])
        k.op("dve", lambda e: e.tensor_scalar(out=nega[:, :], in0=nega[:, :], scalar1=-1.0, scalar2=None, op0=ALU.mult),
             reads=[nega], writes=[nega])
        for c in range(32):
            pd = P[c % 2]
            for kc in range(8):
                mm(k, pd, pd[:, 0:32], hT, hT[:, kc, c * 128:(c + 1) * 128], wdt, wdt[:, kc, :], kc == 0, kc == 7)
            k.op("dve", lambda e: e.tensor_tensor(out=xb[:, c, :], in0=pd[:, 0:32], in1=dtb_row[:, :], op=ALU.add),
                 reads=[pd, dtb_row], writes=[xb])
        k.op("dve", lambda e: e.scalar_tensor_tensor(out=ab[:, :, :], in0=xb[:, :, :], scalar=-1.0, in1=xb[:, :, :],
                                                     op0=ALU.mult, op1=ALU.max), reads=[xb], writes=[ab])
        k.op("act", lambda e: e.activation(out=ab[:, :, :], in_=ab[:, :, :], func=AF.Exp, scale=-1.0), reads=[ab], writes=[ab])
        k.op("act", lambda e: e.activation(out=ab[:, :, :], in_=ab[:, :, :], func=AF.Ln, bias=1.0), reads=[ab], writes=[ab])
        k.op("dve", lambda e: e.scalar_tensor_tensor(out=dt_tok[:, :, :], in0=xb[:, :, :], scalar=0.0, in1=ab[:, :, :],
                                                     op0=ALU.max, op1=ALU.add), reads=[xb, ab], writes=[dt_tok])
        k.op("dve", lambda e: e.tensor_tensor(out=a_tok[:, :, :], in0=dt_tok[:, :, :], in1=b3(nega[:, :], [128, 32, 32], 1),
                                              op=ALU.mult), reads=[dt_tok, nega], writes=[a_tok])
        for hf in range(2):
            pc = P[2 + hf]
            mm(k, pc, pc[:, :], C["tri"], C["tri"][:, :], a_tok, a_tok[:, 16 * hf:16 * hf + 16, :].rearrange("p c h -> p (c h)"), True, True)
            k.op("dve", lambda e: e.tensor_copy(acs_tok[:, 16 * hf:16 * hf + 16, :].rearrange("p c h -> p (c h)"), pc[:, :]),
                 reads=[pc], writes=[acs_tok])
        for c4 in range(8):
            pc = P[4 + c4 % 2]
            for cc in range(4):
                c = c4 * 4 + cc
                mm(k, pc, pc[0:32, cc * 128:(cc + 1) * 128], a_tok, a_tok[:, c, :], C["tri"], C["tri"][:, :], True, True)
            k.op("act", lambda e: e.activation(out=acsT_sb[:, c4 * 512:(c4 + 1) * 512], in_=pc[0:32, :], func=AF.Copy),
                 reads=[pc], writes=[acsT_sb])
        k.dma("sp", acsT_d, acsT_d.t.ap(), acsT_sb, acsT_sb[:, :])


def ssd_phase(k, C, P, PT, hT, w_in, dt_tok, acs_tok, acsT_d, ynT_d):
    vec = C["vec"]
    winv = w_in.t.ap().rearrange("(c p) m -> p c m", p=128)
    ynv = ynT_d.t.ap().rearrange("(j p) t -> p j t", p=128)
    k.barrier()
    with ExitStack() as st:
        wz = k.sbuf("wz", [128, 8, 512], BF16, st)
        wx = k.sbuf("wx", [128, 8, 512], BF16, st)
        wB = k.sbuf("wB", [128, 8, 128], BF16, st)
        wC = k.sbuf("wC", [128, 8, 128], BF16, st)
        diag = k.sbuf("diag", [128, 24, 128], BF16, st)
        dsk = k.sbuf("dsk", [128, 4, 128], BF16, st)
        rawb = [k.sbuf(f"rawb{i}", [128, 6, 515], BF16, st) for i in range(2)]
        xsT = [k.sbuf(f"xsT{i}", [128, 4, 512], BF16, st) for i in range(2)]
        BT = [k.sbuf(f"BT{i}", [128, 512], BF16, st) for i in range(2)]
        CT = [k.sbuf(f"CT{i}", [128, 512], BF16, st) for i in range(2)]
        sz = [k.sbuf(f"sz{i}", [128, 4, 512], BF16, st) for i in range(2)]
        xdt = [k.sbuf(f"xdt{i}", [128, 512], BF16, st) for i in range(2)]
        xdtd = [k.sbuf(f"xdtd{i}", [128, 512], BF16, st) for i in range(2)]
        Btok = [k.sbuf(f"Btok{i}", [128, 128], BF16, st) for i in range(2)]
        acsrow = [k.sbuf(f"acsrow{i}", [128, 8, 128], F32, st) for i in range(2)]
        Dm = k.sbuf("Dm", [128, 8, 128], F32, st)
        Eb = k.sbuf("Eb", [128, 8, 128], BF16, st)
        MT = [k.sbuf(f"MT{i}", [128, 8, 128], BF16, st) for i in range(2)]
        Eacs = k.sbuf("Eacs", [128, 8, 128], BF16, st)
        Cp = [k.sbuf(f"Cp{i}", [128, 8, 128], BF16, st) for i in range(2)]
        CBm = [k.sbuf(f"CBm{i}", [128, 128], BF16, st) for i in range(2)]
        S32 = k.sbuf("S32", [128, 512], F32, st)
        Sbf = [k.sbuf(f"Sbf{i}", [128, 512], BF16, st) for i in range(2)]
        dd = k.sbuf("dd", [128, 8], F32, st)
        dsb = k.sbuf("dsb", [128, 8], F32, st)
        cdb = [k.sbuf(f"cdb{i}", [128, 8], F32, st) for i in range(2)]
        yv = k.sbuf("yv", [128, 4, 512], F32, st)
        sqy = k.sbuf("sqy", [128, 4, 512], BF16, st)
        lny = k.sbuf("lny", [128, 512], F32, st)
        rsy = k.sbuf("rsy", [128, 512], F32, st)
        ynb = [k.sbuf(f"ynb{i}", [128, 4, 512], BF16, st) for i in range(2)]
        st_ = {"pcnt": 0, "sidx": 0}

        def cjof(g, j):
            return 4 * g + j if j < 4 else (16 + g if j == 4 else 20 + g)

        def pro_part(g, tb, part):
            tc = slice(tb * 512, (tb + 1) * 512)
            rb = rawb[tb % 2]
            if part in (0, 1):
                for j in ((0, 1, 2, 3) if part == 0 else (4, 5)):
                    pj = P[st_["pcnt"] % 2]
                    st_["pcnt"] += 1
                    for c in range(8):
                        if j < 4:
                            mm(k, pj, pj[:, :], wx, wx[:, c, j * 128:(j + 1) * 128], hT, hT[:, c, tc], c == 0, c == 7)
                        else:
                            wb_ = wB if j == 4 else wC
                            mm(k, pj, pj[:, :], wb_, wb_[:, c, :], hT, hT[:, c, tc], c == 0, c == 7)
                    k.op("act", lambda e: e.activation(out=rb[:, j, 3:515], in_=pj[:, :], func=AF.Copy), reads=[pj], writes=[rb])
                if part == 1:
                    if tb == 0:
                        k.op("dve", lambda e: e.memset(rb[:, :, 0:3], 0.0), writes=[rb])
                    else:
                        rp = rawb[(tb - 1) % 2]
                        k.op("dve", lambda e: e.tensor_copy(rb[:, :, 0:3], rp[:, :, 512:515]), reads=[rp], writes=[rb])
            if part in (1, 2):
                szb = sz[tb % 2]
                for j in ((0, 1) if part == 1 else (2, 3)):
                    pj = P[st_["pcnt"] % 2]
                    st_["pcnt"] += 1
                    for c in range(8):
                        mm(k, pj, pj[:, :], wz, wz[:, c, j * 128:(j + 1) * 128], hT, hT[:, c, tc], c == 0, c == 7)
                    k.op("act", lambda e: e.activation(out=szb[:, j, :], in_=pj[:, :], func=AF.Silu), reads=[pj], writes=[szb])
            if part == 2:
                for j in range(6):
                    cj = cjof(g, j)
                    pj = P[st_["pcnt"] % 2]
                    st_["pcnt"] += 1
                    for kk in range(4):
                        mm(k, pj, pj[:, :], diag, diag[:, j * 4 + kk, :], rb, rb[:, j, kk:kk + 512], kk == 0, kk == 3)
                    if j < 4:
                        dstb, dsta = xsT[tb % 2], xsT[tb % 2][:, j, :]
                    elif j == 4:
                        dstb, dsta = BT[tb % 2], BT[tb % 2][:, :]
                    else:
                        dstb, dsta = CT[tb % 2], CT[tb % 2][:, :]
                    k.op("act", lambda e: e.activation(out=dsta, in_=pj[:, :], func=AF.Silu, bias=vec[:, V_CB + cj:V_CB + cj + 1]),
                         reads=[pj, vec], writes=[dstb])

        def ctx(g, c):
            tb, cc = c // 4, c % 4
            return dict(tb=tb, cs=slice(cc * 128, (cc + 1) * 128), xs_=xsT[tb % 2], bt_=BT[tb % 2], ct_=CT[tb % 2],
                        ar=acsrow[c % 2], xd=xdt[c % 2], xdd=xdtd[c % 2], bt=Btok[c % 2], cd_=cdb[c % 2],
                        cbm=CBm[c % 2], mt=MT[c % 2], cp=Cp[c % 2],
                        dts=dt_tok[:, c, 8 * g:8 * g + 8], acs_c=acs_tok[:, c, 8 * g:8 * g + 8])

        def f_dma_pe(g, c):
            x = ctx(g, c)
            ar, cs, xs_, bt_, ct_ = x["ar"], x["cs"], x["xs_"], x["bt_"], x["ct_"]
            src = bass.AP(acsT_d.t, (8 * g) * S + c * 128, [[0, 128], [S, 8], [1, 128]])
            k.dma("sp", ar, ar[:, :, :], acsT_d, src)
            for j in range(4):
                k.op("pe", lambda e: e.transpose(PT[:, j * 128:(j + 1) * 128], xs_[:, j, cs], C["ident"][:, :]),
                     reads=[xs_, C["ident"]], writes=[PT], sig=False)
            k.op("pe", lambda e: e.transpose(PT[:, 512:640], bt_[:, cs], C["ident"][:, :]),
                 reads=[bt_, C["ident"]], writes=[PT])
            mm(k, P[2], P[2][:, 0:128], bt_, bt_[:, cs], ct_, ct_[:, cs], True, True)

        def f_1(g, c):
            x = ctx(g, c)
            ar, xd, bt, cd_ = x["ar"], x["xd"], x["bt"], x["cd_"]
            k.op("dve", lambda e: e.tensor_tensor(out=xd[:, :].rearrange("p (h q) -> p h q", h=8),
                                                  in0=PT[:, 0:512].rearrange("p (h q) -> p h q", h=8),
                                                  in1=b3(x["dts"], [128, 8, 64], 2), op=ALU.mult),
                 reads=[PT, dt_tok], writes=[xd])
            k.op("act", lambda e: e.activation(out=bt[:, :], in_=PT[:, 512:640], func=AF.Copy), reads=[PT], writes=[bt])
            k.op("dve", lambda e: e.tensor_tensor(out=dd[:, :], in0=ar[:, :, 127], in1=x["acs_c"], op=ALU.subtract),
                 reads=[ar, acs_tok], writes=[dd])
            k.op("act", lambda e: e.activation(out=dsb[:, :], in_=dd[:, :], func=AF.Exp), reads=[dd], writes=[dsb])
            k.op("act", lambda e: e.activation(out=cd_[:, :], in_=ar[:, :, 127], func=AF.Exp), reads=[ar], writes=[cd_])
            k.op("dve", lambda e: e.tensor_tensor(out=Dm[:, :, :], in0=ar[:, :, :], in1=b3(x["acs_c"], [128, 8, 128], 2),
                                                  op=ALU.subtract), reads=[ar, acs_tok], writes=[Dm])
            k.op("dve", lambda e: e.tensor_scalar(out=Dm[:, :, :], in0=Dm[:, :, :], scalar1=0.0, scalar2=None, op0=ALU.min),
                 reads=[Dm], writes=[Dm])
            k.op("act", lambda e: e.activation(out=Eb[:, :, :], in_=Dm[:, :, :], func=AF.Exp), reads=[Dm], writes=[Eb])
            if c > 0:
                k.op("act", lambda e: e.activation(out=Eacs[:, :, :], in_=ar[:, :, :], func=AF.Exp), reads=[ar], writes=[Eacs])
            xdd, cbm = x["xdd"], x["cbm"]
            k.op("dve", lambda e: e.tensor_tensor(out=xdd[:, :].rearrange("p (h q) -> p h q", h=8),
                                                  in0=xd[:, :].rearrange("p (h q) -> p h q", h=8),
                                                  in1=b3(dsb[:, :], [128, 8, 64], 2), op=ALU.mult),
                 reads=[xd, dsb], writes=[xdd])
            k.op("dve", lambda e: e.tensor_tensor(out=cbm[:, :], in0=P[2][:, 0:128], in1=C["tri"][:, :], op=ALU.mult),
                 reads=[P[2], C["tri"]], writes=[cbm])

        def f_2(g, c):
            x = ctx(g, c)
            mt, cp, cbm, ct_, cs = x["mt"], x["cp"], x["cbm"], x["ct_"], x["cs"]
            k.op("dve", lambda e: e.tensor_tensor(out=mt[:, :, :], in0=Eb[:, :, :], in1=b3(cbm[:, :], [128, 8, 128], 1),
                                                  op=ALU.mult), reads=[Eb, cbm], writes=[mt])
            if c > 0:
                k.op("dve", lambda e: e.tensor_tensor(out=cp[:, :, :], in0=Eacs[:, :, :], in1=b3(ct_[:, cs], [128, 8, 128], 1),
                                                      op=ALU.mult), reads=[Eacs, ct_], writes=[cp])

        def b_act0(g, c):
            if c > 0:
                sb_prev = Sbf[st_["sidx"] % 2]
                k.op("act", lambda e: e.activation(out=sb_prev[:, :], in_=S32[:, :], func=AF.Copy), reads=[S32], writes=[sb_prev])

        def b_pe(g, c):
            x = ctx(g, c)
            cs, xs_, xd, xdd, bt, mt, cp = x["cs"], x["xs_"], x["xd"], x["xdd"], x["bt"], x["mt"], x["cp"]
            sb_prev = Sbf[st_["sidx"] % 2]
            pY = P[3 + c % 2]
            for hp in range(4):
                ocol = slice(hp * 128, (hp + 1) * 128)
                mm(k, pY, pY[:, ocol], dsk, dsk[:, hp, :], xs_, xs_[:, hp, cs], True, False)
                for hh in range(2):
                    h = 2 * hp + hh
                    orow = slice(64 * hh, 64 * hh + 64)
                    hs = slice(h * 64, (h + 1) * 64)
                    mm(k, pY, pY[orow, ocol], xd, xd[:, hs], mt, mt[:, h, :], False, c == 0)
                    if c > 0:
                        mm(k, pY, pY[orow, ocol], sb_prev, sb_prev[:, hs], cp, cp[:, h, :], False, True)
            if c < 31:
                mm(k, P[5], P[5][:, :], bt, bt[:, :], xdd, xdd[:, :], True, True)

        def b_dve(g, c):
            x = ctx(g, c)
            cs, cd_ = x["cs"], x["cd_"]
            pY = P[3 + c % 2]
            szb = sz[x["tb"] % 2]
            k.op("dve", lambda e: e.tensor_tensor(out=yv[:, :, cs], in0=pY[:, :].rearrange("p (h l) -> p h l", h=4),
                                                  in1=szb[:, :, cs], op=ALU.mult), reads=[pY, szb], writes=[yv])
            if c < 31:
                if c == 0:
                    k.op("dve", lambda e: e.tensor_copy(S32[:, :], P[5][:, :]), reads=[P[5]], writes=[S32])
                else:
                    k.op("dve", lambda e: e.tensor_tensor(out=S32[:, :].rearrange("p (h q) -> p h q", h=8),
                                                          in0=S32[:, :].rearrange("p (h q) -> p h q", h=8),
                                                          in1=b3(cd_[:, :], [128, 8, 64], 2), op=ALU.mult),
                         reads=[S32, cd_], writes=[S32])
                    k.op("dve", lambda e: e.tensor_tensor(out=S32[:, :], in0=S32[:, :], in1=P[5][:, :], op=ALU.add),
                         reads=[S32, P[5]], writes=[S32])
                st_["sidx"] += 1

        def epilogue(g, tb):
            tc = slice(tb * 512, (tb + 1) * 512)
            k.op("act", lambda e: e.activation(out=sqy[:, :, :], in_=yv[:, :, :], func=AF.Square), reads=[yv], writes=[sqy])
            for hp in range(4):
                mm(k, P[6], P[6][:, :], C["ones"], C["ones"][:, :], sqy, sqy[:, hp, :], hp == 0, hp == 3)
            k.op("act", lambda e: e.activation(out=lny[:, :], in_=P[6][:, :], func=AF.Ln, scale=1.0 / 512, bias=C["eps"][:, 0:1]),
                 reads=[P[6], C["eps"]], writes=[lny])
            k.op("act", lambda e: e.activation(out=rsy[:, :], in_=lny[:, :], func=AF.Exp, scale=-0.5), reads=[lny], writes=[rsy])
            yb = ynb[(g * 8 + tb) % 2]
            for hp in range(4):
                ncol_ = V_SSN + 4 * g + hp
                k.op("dve", lambda e: e.scalar_tensor_tensor(out=yb[:, hp, :], in0=yv[:, hp, :], scalar=vec[:, ncol_:ncol_ + 1],
                                                             in1=rsy[:, :], op0=ALU.mult, op1=ALU.mult),
                     reads=[yv, vec, rsy], writes=[yb])
            k.dma("pool", ynT_d, ynv[:, 4 * g:4 * g + 4, tc], yb, yb[:, :, :])

        for g in range(4):
            k.dma("pool", wx, wx[:, :, :], w_in, winv[:, :, XOFF + 512 * g:XOFF + 512 * g + 512])
            k.dma("pool", wB, wB[:, :, :], w_in, winv[:, :, BOFF + 128 * g:BOFF + 128 * g + 128])
            k.dma("pool", wC, wC[:, :, :], w_in, winv[:, :, COFF + 128 * g:COFF + 128 * g + 128])
            k.dma("pool", wz, wz[:, :, :], w_in, winv[:, :, ZOFF + 512 * g:ZOFF + 512 * g + 512])
            for j in range(6):
                cj = cjof(g, j)
                for kk in range(4):
                    col = V_CW + cj * 4 + kk
                    k.op("dve", lambda e: e.tensor_scalar(out=diag[:, j * 4 + kk, :], in0=C["ident"][:, :], scalar1=vec[:, col:col + 1],
                                                          scalar2=None, op0=ALU.mult), reads=[C["ident"], vec], writes=[diag])
            for hp in range(4):
                col = V_DSK + 4 * g + hp
                k.op("dve", lambda e: e.tensor_scalar(out=dsk[:, hp, :], in0=C["ident"][:, :], scalar1=vec[:, col:col + 1],
                                                      scalar2=None, op0=ALU.mult), reads=[C["ident"], vec], writes=[dsk])
            for part in range(3):
                pro_part(g, 0, part)
            f_dma_pe(g, 0)
            f_1(g, 0)
            f_2(g, 0)
            for tb in range(8):
                for cc in range(4):
                    c = tb * 4 + cc
                    b_act0(g, c)
                    if c + 1 < 32:
                        f_dma_pe(g, c + 1)
                    b_pe(g, c)
                    if c + 1 < 32:
                        f_1(g, c + 1)
                    b_dve(g, c)
                    if c + 1 < 32:
                        f_2(g, c + 1)
                    if tb + 1 < 8 and cc < 3:
                        pro_part(g, tb + 1, cc)
                epilogue(g, tb)


def merge_phase(k, C, P, hT, w_in, w_ab, w_sb, w_out, x1T_d, oT_d, ynT_d, x2T_d):
    winv = w_in.t.ap().rearrange("(c p) m -> p c m", p=128)
    wabv = w_ab.t.ap().rearrange("(c p) m -> p c m", p=128)
    wsbv = w_sb.t.ap().rearrange("(c p) m -> p c m", p=128)
    woutv = w_out.t.ap().rearrange("(c p) m -> p c m", p=128)
    x1v = x1T_d.t.ap().rearrange("(c p) t -> p c t", p=128)
    x2v = x2T_d.t.ap().rearrange("(c p) t -> p c t", p=128)
    oTv = oT_d.t.ap().rearrange("(j p) t -> p j t", p=128)
    ynv = ynT_d.t.ap().rearrange("(j p) t -> p j t", p=128)
    k.barrier()
    with ExitStack() as st:
        wout = k.sbuf("wout", [128, 8, 1024], BF16, st)
        x1g = [k.sbuf(f"x1g{i}", [128, 8, 512], F32, st) for i in range(2)]
        yng = [k.sbuf(f"yng{i}", [128, 16, 512], BF16, st) for i in range(1)]
        og = [k.sbuf(f"og{i}", [128, 4, 512], BF16, st) for i in range(2)]
        merged = k.sbuf("merged", [128, 8, 512], BF16, st)
        wab = [k.sbuf(f"wab{i}", [128, 4, 256], BF16, st) for i in range(2)]
        wsb = [k.sbuf(f"wsb{i}", [128, 16, 256], BF16, st) for i in range(2)]
        wga = [k.sbuf(f"wga{i}", [128, 8, 256], BF16, st) for i in range(2)]
        wgs = [k.sbuf(f"wgs{i}", [128, 8, 256], BF16, st) for i in range(2)]
        sga = k.sbuf("sga", [128, 512], F32, st)
        sgs = k.sbuf("sgs", [128, 512], F32, st)
        t1 = k.sbuf("t1", [128, 512], F32, st)
        t2 = k.sbuf("t2", [128, 512], F32, st)
        k.dma("pool", wout, wout[:, :, :], w_out, woutv[:, :, :])

        def loads(G):
            tc = slice(G * 512, (G + 1) * 512)
            k.dma("sp", x1g[G % 2], x1g[G % 2][:, :, :], x1T_d, x1v[:, :, tc])
            k.dma("sp", og[G % 2], og[G % 2][:, :, :], oT_d, oTv[:, :, tc])

        loads(0)
        wcnt = 0
        for G in range(8):
            tc = slice(G * 512, (G + 1) * 512)
            if G + 1 < 8:
                loads(G + 1)
            X = x1g[G % 2]
            yn_ = yng[0]
            k.dma("sp", yn_, yn_[:, :, :], ynT_d, ynv[:, :, tc])
            o_ = og[G % 2]
            for mp in range(4):
                ws = wcnt % 2
                wcnt += 1
                cs = slice(mp * 256, (mp + 1) * 256)
                k.dma("pool", wga[ws], wga[ws][:, :, :], w_in, winv[:, :, GAOFF + mp * 256:GAOFF + (mp + 1) * 256])
                k.dma("pool", wgs[ws], wgs[ws][:, :, :], w_in, winv[:, :, GSOFF + mp * 256:GSOFF + (mp + 1) * 256])
                k.dma("pool", wab[ws], wab[ws][:, :, :], w_ab, wabv[:, :, cs])
                k.dma("pool", wsb[ws], wsb[ws][:, :, :], w_sb, wsbv[:, :, cs])
                for mm_ in range(2):
                    m = mp * 2 + mm_
                    mc = slice(mm_ * 128, (mm_ + 1) * 128)
                    pga, pgs, pa, ps_ = P[0], P[1], P[2], P[3]
                    for c in range(8):
                        mm(k, pga, pga[:, :], wga[ws], wga[ws][:, c, mc], hT, hT[:, c, tc], c == 0, c == 7)
                    for c in range(8):
                        mm(k, pgs, pgs[:, :], wgs[ws], wgs[ws][:, c, mc], hT, hT[:, c, tc], c == 0, c == 7)
                    k.op("act", lambda e: e.activation(out=sga[:, :], in_=pga[:, :], func=AF.Sigmoid), reads=[pga], writes=[sga])
                    k.op("act", lambda e: e.activation(out=sgs[:, :], in_=pgs[:, :], func=AF.Sigmoid), reads=[pgs], writes=[sgs])
                    for c in range(4):
                        mm(k, pa, pa[:, :], wab[ws], wab[ws][:, c, mc], o_, o_[:, c, :], c == 0, c == 3)
                    for c in range(16):
                        mm(k, ps_, ps_[:, :], wsb[ws], wsb[ws][:, c, mc], yn_, yn_[:, c, :], c == 0, c == 15)
                    k.op("dve", lambda e: e.tensor_tensor(out=t1[:, :], in0=sga[:, :], in1=pa[:, :], op=ALU.mult),
                         reads=[sga, pa], writes=[t1])
                    k.op("dve", lambda e: e.tensor_tensor(out=t2[:, :], in0=sgs[:, :], in1=ps_[:, :], op=ALU.mult),
                         reads=[sgs, ps_], writes=[t2])
                    k.op("dve", lambda e: e.tensor_tensor(out=merged[:, m, :], in0=t1[:, :], in1=t2[:, :], op=ALU.add),
                         reads=[t1, t2], writes=[merged])
            for m in range(8):
                po = P[4 + m % 3]
                for c in range(8):
                    mm(k, po, po[:, :], wout, wout[:, c, m * 128:(m + 1) * 128], merged, merged[:, c, :], c == 0, c == 7)
                k.op("dve", lambda e: e.tensor_tensor(out=X[:, m, :], in0=po[:, :], in1=X[:, m, :], op=ALU.add),
                     reads=[po, X], writes=[X])
            k.dma("sp", x2T_d, x2v[:, :, tc], X, X[:, :, :])


def build(debug=False, upto=5):
    nc = bass.Bass("TRN2", target_bir_lowering=False)
    dk = "ExternalOutput" if debug else "Internal"
    with ExitStack() as stack:
        k = KB(nc, stack)
        xT = k.dram("xT", [D, S], F32, "ExternalInput")
        wg1 = k.dram("w_g1", [D, DFF], F32, "ExternalInput")
        wu1 = k.dram("w_u1", [D, DFF], F32, "ExternalInput")
        wd1 = k.dram("w_d1", [DFF, D], F32, "ExternalInput")
        wg2 = k.dram("w_g2", [D, DFF], F32, "ExternalInput")
        wu2 = k.dram("w_u2", [D, DFF], F32, "ExternalInput")
        wd2 = k.dram("w_d2", [DFF, D], F32, "ExternalInput")
        w_in = k.dram("w_in", [D, NCOLS], F32, "ExternalInput")
        w_ab = k.dram("w_ab", [512, D], F32, "ExternalInput")
        w_sb = k.dram("w_sb", [2048, D], F32, "ExternalInput")
        w_out = k.dram("w_out", [D, D], F32, "ExternalInput")
        vecs = k.dram("vecs", [128, NV], F32, "ExternalInput")
        dtb_d = k.dram("dtb", [32], F32, "ExternalInput")
        alog_d = k.dram("alog", [32], F32, "ExternalInput")
        outT = k.dram("outT", [D, S], F32, "ExternalOutput")
        x1T_d = k.dram("x1T", [D, S], F32, dk)
        x2T_d = k.dram("x2T", [D, S], F32, dk)
        oT_d = k.dram("oT", [512, S], BF16, dk)
        ynT_d = k.dram("ynT", [2048, S], BF16, dk)
        acsT_d = k.dram("acsT", [32, S], F32, dk)
        hT_d = k.dram("hT_dbg", [D, S], BF16, dk) if debug else None

        P = [k.psum(f"ps{i}", [128, 512], F32) for i in range(7)]
        PT = k.psum("pst", [128, 1024], BF16)

        C = {}
        vec = k.sbuf("vec", [128, NV], F32)
        C["vec"] = vec
        k.dma("sp", vec, vec[:, :], vecs, vecs.t.ap())
        ones = k.sbuf("ones_bf", [128, 128], BF16)
        k.op("dve", lambda e: e.memset(ones[:, :], 1.0), writes=[ones])
        C["ones"] = ones
        bd = k.sbuf("bd_bf", [128, 128], BF16)
        k.op("dve", lambda e: e.memset(bd[:, :], 0.0), writes=[bd])
        k.op("dve", lambda e: e.memset(bd[0:64, 0:64], 1.0), writes=[bd])
        k.op("dve", lambda e: e.memset(bd[64:128, 64:128], 1.0), writes=[bd])
        C["bd"] = bd
        epsb = k.sbuf("eps_c", [128, 1], F32)
        k.op("dve", lambda e: e.memset(epsb[:, :], EPS), writes=[epsb])
        C["eps"] = epsb
        identf = k.sbuf("identf", [128, 128], F32)
        k.op("pool", lambda e: e.memset(identf[:, :], 0.0), writes=[identf])
        k.op("pool", lambda e: e.affine_select(out=identf[:, :], in_=identf[:, :], pattern=[[-1, 128]],
                                               compare_op=ALU.not_equal, fill=1.0, base=0, channel_multiplier=1),
             reads=[identf], writes=[identf])
        ident = k.sbuf("ident_bf", [128, 128], BF16)
        k.op("dve", lambda e: e.tensor_copy(ident[:, :], identf[:, :]), reads=[identf], writes=[ident])
        C["ident"] = ident
        tri = k.sbuf("tri_f", [128, 128], F32)
        k.op("pool", lambda e: e.memset(tri[:, :], 1.0), writes=[tri])
        k.op("pool", lambda e: e.affine_select(out=tri[:, :], in_=tri[:, :], pattern=[[1, 128]],
                                               compare_op=ALU.is_ge, fill=0.0, base=0, channel_multiplier=-1),
             reads=[tri], writes=[tri])
        C["tri"] = tri
        r2i = k.sbuf("r2i", [128, 256], I32)
        R2 = k.sbuf("R2", [128, 256], F32)
        k.op("pool", lambda e: e.iota(r2i[:, 0:128], pattern=[[1, 128]], base=0, channel_multiplier=-1), writes=[r2i])
        k.op("pool", lambda e: e.iota(r2i[:, 128:256], pattern=[[1, 128]], base=128, channel_multiplier=-1),
             reads=[r2i], writes=[r2i])
        k.op("dve", lambda e: e.tensor_copy(R2[:, :], r2i[:, :]), reads=[r2i], writes=[R2])
        k.op("pool", lambda e: e.affine_select(out=R2[:, 0:128], in_=R2[:, 0:128], pattern=[[1, 128]],
                                               compare_op=ALU.is_ge, fill=BIG, base=0, channel_multiplier=-1),
             reads=[R2], writes=[R2])
        k.op("pool", lambda e: e.affine_select(out=R2[:, 128:256], in_=R2[:, 128:256], pattern=[[-1, 128]],
                                               compare_op=ALU.is_ge, fill=BIG, base=0, channel_multiplier=1),
             reads=[R2], writes=[R2])
        C["R2"] = R2
        k.op("dve", lambda e: e.tensor_scalar(out=vec[:, V_GQ:V_GQ + 1], in0=vec[:, V_GQ:V_GQ + 1], scalar1=0.125, scalar2=None,
                                              op0=ALU.mult), reads=[vec], writes=[vec])

        hT = k.sbuf("hT", [128, 8, S], BF16)

        last = x1T_d
        ffn_phase(k, C, P, "f1", xT, x1T_d, wg1, wu1, wd1, V_N1, mix=(V_NM, hT))
        if debug:
            hv = hT_d.t.ap().rearrange("(c p) t -> p c t", p=128)
            k.dma("sp", hT_d, hv, hT, hT[:, :, :])
        outs = [x1T_d]
        if upto >= 2:
            attn_phase(k, C, P, hT, w_in, oT_d)
            outs.append(oT_d)
        if upto >= 3:
            with ExitStack() as sst:
                dt_tok = k.sbuf("dt_tok", [128, 32, 32], F32, sst)
                acs_tok = k.sbuf("acs_tok", [128, 32, 32], F32, sst)
                ssd_prep(k, C, P, hT, w_in, dtb_d, alog_d, dt_tok, acs_tok, acsT_d)
                ssd_phase(k, C, P, PT, hT, w_in, dt_tok, acs_tok, acsT_d, ynT_d)
            outs.append(ynT_d)
        if upto >= 4:
            merge_phase(k, C, P, hT, w_in, w_ab, w_sb, w_out, x1T_d, oT_d, ynT_d, x2T_d)
            outs.append(x2T_d)
        if upto >= 5:
            ffn_phase(k, C, P, "f2", x2T_d, outT, wg2, wu2, wd2, V_N2, mix=None)
            outs.append(outT)
        if debug:
            outs.append(hT_d)
        k.finish("sp", outs)
        build.stats = (k.n_inst, k.n_wait, dict(k.ecnt), len(k.sems))
    return nc


def prep_inputs(inputs):
    f = lambda a: np.ascontiguousarray(np.asarray(a, dtype=np.float32))
    x = f(inputs["x"])
    vec = np.zeros((128, NV), np.float32)
    pc = lambda v: f(v).reshape(-1, 128).T
    vec[:, V_N1:V_N1 + 8] = pc(inputs["ffn1_norm"][0])
    vec[:, V_NM:V_NM + 8] = pc(inputs["mix_norm"][0])
    vec[:, V_N2:V_N2 + 8] = pc(inputs["ffn2_norm"][0])
    vec[:, V_GQ] = np.tile(f(inputs["q_norm"][0]), 2)
    vec[:, V_GK] = np.tile(f(inputs["k_norm"][0]), 2)
    cw = f(inputs["conv_w"][0])
    vec[:, V_CW:V_CW + 96] = cw.reshape(4, 24, 128).transpose(2, 1, 0).reshape(128, 96)
    vec[:, V_CB:V_CB + 24] = pc(inputs["conv_b"][0])
    vec[:, V_DSK:V_DSK + 16] = pc(np.repeat(f(inputs["d_skip"][0]), 64))
    vec[:, V_SSN:V_SSN + 16] = pc(inputs["ssd_norm"][0])
    shared = {
        "w_g1": f(inputs["ffn1_w_gate"][0]), "w_u1": f(inputs["ffn1_w_up"][0]), "w_d1": f(inputs["ffn1_w_down"][0]),
        "w_g2": f(inputs["ffn2_w_gate"][0]), "w_u2": f(inputs["ffn2_w_up"][0]), "w_d2": f(inputs["ffn2_w_down"][0]),
        "w_in": f(inputs["w_in"][0]), "w_ab": f(inputs["w_attn_branch"][0]), "w_sb": f(inputs["w_ssd_branch"][0]),
        "w_out": f(inputs["w_out"][0]), "vecs": vec, "dtb": f(inputs["dt_bias"][0]), "alog": f(inputs["a_log"][0]),
    }
    in_maps = []
    for b in range(x.shape[0]):
        m = dict(shared)
        m["xT"] = np.ascontiguousarray(x[b].T)
        in_maps.append(m)
    return in_maps


_NC = None


def kernel(**inputs):
    global _NC
    if _NC is None:
        _NC = build()
    in_maps = prep_inputs(inputs)
    res = run_bass_kernel_spmd(_NC, in_maps, core_ids=list(range(8)))
    out = np.stack([np.ascontiguousarray(np.asarray(r["outT"]).T) for r in res.results], axis=0)
    return out.astype(np.float32)
```
